# Optimizing a Trainium2 kernel written in Bass

```python
import math
import jax, jax.numpy as jnp
from jax import lax
import numpy as np

D_MODEL = 1024
BATCH = 8
SEQ = 2048
DEPTH = 4

HEAD_DIM = 64
BLOCK = 128
EPS = 1e-6
ROPE_BASE = 10000.0
A_HEADS = 6
A_KV_HEADS = 2
WINDOW = 128
B_HEADS = 6
B_KV_HEADS = 2
GRID_W = 64
C_HEADS = 4
C_Q_RANK = 256
C_KV_RANK = 128
C_NOPE = 64
C_ROPE = 32
C_V = 64
D_FF = 4 * D_MODEL

A_Q = A_HEADS * HEAD_DIM
A_KV = A_KV_HEADS * HEAD_DIM
B_Q = B_HEADS * HEAD_DIM
B_KV = B_KV_HEADS * HEAD_DIM
A_COLS = A_Q + 2 * A_KV
B_COLS = B_Q + 2 * B_KV
C_COLS = C_Q_RANK + C_KV_RANK + C_ROPE
IN_COLS = A_COLS + B_COLS + C_COLS
MIX_WIDTH = A_Q + B_Q + C_HEADS * C_V

kernel_name = "hybrid_parallel_heads_encoder"


def rmsnorm(x, g):
    xf = x.astype(jnp.float32)
    y = xf * lax.rsqrt(jnp.mean(xf * xf, axis=-1, keepdims=True) + EPS)
    return (y * g.astype(jnp.float32)).astype(x.dtype)


def rope_tables(pos, dim):
    inv = ROPE_BASE ** (-jnp.arange(0, dim, 2, dtype=jnp.float32) / dim)
    ang = pos.astype(jnp.float32)[:, None] * inv[None, :]
    ang = jnp.concatenate([ang, ang], axis=-1)
    return jnp.cos(ang), jnp.sin(ang)


def apply_rope(x, cos, sin):
    half = x.shape[-1] // 2
    x1, x2 = x[..., :half], x[..., half:]
    rot = jnp.concatenate([-x2, x1], axis=-1)
    return (x.astype(jnp.float32) * cos + rot.astype(jnp.float32) * sin).astype(x.dtype)


def alibi_slopes(n):
    return 2.0 ** (-8.0 * jnp.arange(1, n + 1, dtype=jnp.float32) / n)


def windowed_gqa_sink(q, k, v, sink):
    b, s, hq, d = q.shape
    hkv = k.shape[2]
    g = hq // hkv
    nb = s // BLOCK
    qb = q.reshape(b, nb, BLOCK, hkv, g, d)
    pad = ((0, 0), (BLOCK, BLOCK), (0, 0), (0, 0))
    kp = jnp.pad(k, pad).reshape(b, nb + 2, BLOCK, hkv, d)
    vp = jnp.pad(v, pad).reshape(b, nb + 2, BLOCK, hkv, d)
    kb = jnp.concatenate([kp[:, :-2], kp[:, 1:-1], kp[:, 2:]], axis=2)
    vb = jnp.concatenate([vp[:, :-2], vp[:, 1:-1], vp[:, 2:]], axis=2)
    sc = jnp.einsum('bnqhgd,bnkhd->bnhgqk', qb, kb).astype(jnp.float32) * (d ** -0.5)
    blk = jnp.arange(nb)[:, None] * BLOCK
    qpos = blk + jnp.arange(BLOCK)[None, :]
    kpos = blk - BLOCK + jnp.arange(3 * BLOCK)[None, :]
    dist = jnp.abs(qpos[:, :, None] - kpos[:, None, :])
    valid = (dist <= WINDOW) & ((kpos >= 0) & (kpos < s))[:, None, :]
    slopes = alibi_slopes(hq).reshape(hkv, g)
    bias = -slopes[None, :, :, None, None] * dist[:, None, None].astype(jnp.float32)
    sc = jnp.where(valid[:, None, None], sc + bias, -1e30)
    sk = sink.astype(jnp.float32).reshape(hkv, g)[:, :, None]
    m = jnp.maximum(jnp.max(sc, axis=-1), sk)
    p = jnp.exp(sc - m[..., None])
    probs = p / (jnp.sum(p, axis=-1) + jnp.exp(sk - m))[..., None]
    out = jnp.einsum('bnhgqk,bnkhd->bnqhgd', probs.astype(v.dtype), vb)
    return out.reshape(b, s, hq * d)


def dense_gqa_blocks(q, k, v):
    b, s, hq, d = q.shape
    hkv = k.shape[2]
    g = hq // hkv
    nb = s // BLOCK
    qb = q.reshape(b, nb, BLOCK, hkv, g, d).transpose(1, 0, 2, 3, 4, 5)

    def one(qblk):
        sc = jnp.einsum('bqhgd,bkhd->bhgqk', qblk, k).astype(jnp.float32) * (d ** -0.5)
        p = jax.nn.softmax(sc, axis=-1).astype(v.dtype)
        return jnp.einsum('bhgqk,bkhd->bqhgd', p, v)

    out = lax.map(one, qb)
    return out.transpose(1, 0, 2, 3, 4, 5).reshape(b, s, hq * d)


def mla_blocks(q_nope, q_rope, k_nope, k_rope, v):
    b, s, h, dn = q_nope.shape
    dr = q_rope.shape[-1]
    nb = s // BLOCK
    scale = (dn + dr) ** -0.5
    qn = q_nope.reshape(b, nb, BLOCK, h, dn).transpose(1, 0, 2, 3, 4)
    qr = q_rope.reshape(b, nb, BLOCK, h, dr).transpose(1, 0, 2, 3, 4)

    def one(args):
        qn_b, qr_b = args
        sc = (jnp.einsum('bqhd,bkhd->bhqk', qn_b, k_nope)
              + jnp.einsum('bqhd,bkd->bhqk', qr_b, k_rope)).astype(jnp.float32) * scale
        p = jax.nn.softmax(sc, axis=-1).astype(v.dtype)
        return jnp.einsum('bhqk,bkhd->bqhd', p, v)

    out = lax.map(one, (qn, qr))
    return out.transpose(1, 0, 2, 3, 4).reshape(b, s, h * v.shape[-1])


def setup_inputs(seed: int = 0) -> dict:
    key = jax.random.key(seed)
    ks = jax.random.split(key, 16)
    f32 = jnp.float32

    def nrm(k, shape, fan_in):
        return jax.random.normal(k, shape, f32) * (fan_in ** -0.5)

    def gain(k, shape):
        return 1.0 + 0.05 * jax.random.normal(k, shape, f32)

    return {
        "x": jax.random.normal(ks[0], (BATCH, SEQ, D_MODEL), f32),
        "attn_norm": gain(ks[1], (DEPTH, D_MODEL)),
        "w_in": nrm(ks[2], (DEPTH, D_MODEL, IN_COLS), D_MODEL),
        "a_sink": 0.5 * jax.random.normal(ks[3], (DEPTH, A_HEADS), f32),
        "b_q_norm": gain(ks[4], (DEPTH, HEAD_DIM)),
        "b_k_norm": gain(ks[5], (DEPTH, HEAD_DIM)),
        "c_q_norm": gain(ks[6], (DEPTH, C_Q_RANK)),
        "c_kv_norm": gain(ks[7], (DEPTH, C_KV_RANK)),
        "w_uq": nrm(ks[8], (DEPTH, C_Q_RANK, C_HEADS * (C_NOPE + C_ROPE)), C_Q_RANK),
        "w_ukv": nrm(ks[9], (DEPTH, C_KV_RANK, C_HEADS * (C_NOPE + C_V)), C_KV_RANK),
        "w_out": nrm(ks[10], (DEPTH, MIX_WIDTH, D_MODEL), MIX_WIDTH),
        "mlp_norm": gain(ks[11], (DEPTH, D_MODEL)),
        "w_ff1": nrm(ks[12], (DEPTH, D_MODEL, D_FF), D_MODEL),
        "w_ff2": nrm(ks[13], (DEPTH, D_FF, D_MODEL), D_FF),
        "final_norm": gain(ks[14], (D_MODEL,)),
    }


def reference(x, attn_norm, w_in, a_sink, b_q_norm, b_k_norm, c_q_norm, c_kv_norm,
              w_uq, w_ukv, w_out, mlp_norm, w_ff1, w_ff2, final_norm):
    b, s, _ = x.shape
    rows = s // GRID_W
    pos = jnp.arange(s)
    row_pos = jnp.repeat(jnp.arange(rows), GRID_W)
    col_pos = jnp.tile(jnp.arange(GRID_W), rows)
    half = HEAD_DIM // 2
    cos_r, sin_r = rope_tables(row_pos, half)
    cos_c, sin_c = rope_tables(col_pos, half)
    cos_m, sin_m = rope_tables(pos, C_ROPE)

    def axial(t):
        tr = apply_rope(t[..., :half], cos_r[:, None, :], sin_r[:, None, :])
        tc = apply_rope(t[..., half:], cos_c[:, None, :], sin_c[:, None, :])
        return jnp.concatenate([tr, tc], axis=-1)

    o1 = A_Q
    o2 = o1 + A_KV
    o3 = o2 + A_KV
    o4 = o3 + B_Q
    o5 = o4 + B_KV
    o6 = o5 + B_KV
    o7 = o6 + C_Q_RANK
    o8 = o7 + C_KV_RANK

    for l in range(DEPTH):
        h = rmsnorm(x, attn_norm[l])
        p = h @ w_in[l]

        qa = p[..., :o1].reshape(b, s, A_HEADS, HEAD_DIM)
        ka = p[..., o1:o2].reshape(b, s, A_KV_HEADS, HEAD_DIM)
        va = p[..., o2:o3].reshape(b, s, A_KV_HEADS, HEAD_DIM)
        out_a = windowed_gqa_sink(qa, ka, va, a_sink[l])

        qb = rmsnorm(p[..., o3:o4].reshape(b, s, B_HEADS, HEAD_DIM), b_q_norm[l])
        kb = rmsnorm(p[..., o4:o5].reshape(b, s, B_KV_HEADS, HEAD_DIM), b_k_norm[l])
        vb = p[..., o5:o6].reshape(b, s, B_KV_HEADS, HEAD_DIM)
        out_b = dense_gqa_blocks(axial(qb), axial(kb), vb)

        cq = rmsnorm(p[..., o6:o7], c_q_norm[l]) @ w_uq[l]
        cq = cq.reshape(b, s, C_HEADS, C_NOPE + C_ROPE)
        q_nope = cq[..., :C_NOPE]
        q_rope = apply_rope(cq[..., C_NOPE:], cos_m[:, None, :], sin_m[:, None, :])
        ckv = rmsnorm(p[..., o7:o8], c_kv_norm[l]) @ w_ukv[l]
        ckv = ckv.reshape(b, s, C_HEADS, C_NOPE + C_V)
        k_nope = ckv[..., :C_NOPE]
        vc = ckv[..., C_NOPE:]
        k_rope = apply_rope(p[..., o8:], cos_m, sin_m)
        out_c = mla_blocks(q_nope, q_rope, k_nope, k_rope, vc)

        mixed = jnp.concatenate([out_a, out_b, out_c], axis=-1)
        x = x + mixed @ w_out[l]

        h2 = rmsnorm(x, mlp_norm[l])
        x = x + jnp.square(jax.nn.relu(h2 @ w_ff1[l])) @ w_ff2[l]

    return rmsnorm(x, final_norm)
```

```python
import math
from contextlib import ExitStack

import numpy as np
import concourse.bass as bass
import concourse.mybir as mybir
from concourse.bass_utils import run_bass_kernel_spmd

F32 = mybir.dt.float32
BF16 = mybir.dt.bfloat16
AF = mybir.ActivationFunctionType
ALU = mybir.AluOpType

S = 2048
D = 1024
DEPTH = 4
NCORES = 8
EPS = 1e-6
TGN = 512
NTG = S // TGN
D_FF = 4096
NG = 8
SC_AB = 0.125
SC_C = 96.0 ** -0.5

def _cols(L):
    m = {}
    o = 0
    for name, n in (("attn", 8 * L), ("mlp", 8 * L), ("final", 8), ("bq", L), ("bk", L),
                    ("cq", 2 * L), ("ckv", L), ("sink", 6 * L)):
        m[name] = o
        o += n
    m["_n"] = o
    return m

CB_ONES = 0
CB_BLK = 128
CB_RM = 256
CB_MASK = 384
CB_COSB = CB_MASK + 6 * 384
CB_SINB = CB_COSB + S
CB_CSC = CB_SINB + S
CB_N = CB_CSC + S


class Buf:
    __slots__ = ("name", "w", "r")

    def __init__(self, name):
        self.name = name
        self.w = None
        self.r = []


class Op:
    __slots__ = ("eng", "fn", "raw", "oth", "needs_inc", "sem", "val", "waits", "is_dma", "known", "idx", "hoist")


class Tracker:
    ENGS = ("pe", "act", "dve", "pool", "sp")

    def __init__(self):
        self.ops = []
        self.last = {e: None for e in self.ENGS}
        self.dma_count = {}
        self.dma_pending = []
        self.all_stores = []

    def _add(self, eng, fn, reads, writes, is_dma, sem):
        o = Op()
        o.eng = eng
        o.fn = fn
        o.is_dma = is_dma
        o.needs_inc = False
        o.sem = sem
        o.val = 0
        o.waits = []
        o.known = None
        o.idx = len(self.ops)
        o.hoist = False
        raw = set()
        oth = set()
        for b in reads:
            if b.w is not None:
                raw.add(b.w)
        for b in writes:
            if b.w is not None:
                oth.add(b.w)
            for r in b.r:
                oth.add(r)
        oth -= raw
        raw.discard(o)
        oth.discard(o)
        o.raw = raw
        o.oth = oth
        for b in reads:
            b.r.append(o)
        for b in writes:
            b.w = o
            b.r = []
        self.ops.append(o)
        if is_dma:
            n = self.dma_count.get(sem, 0) + 1
            self.dma_count[sem] = n
            o.val = 16 * n
            self.dma_pending.append(o)
        else:
            self.last[eng] = o
        return o

    def op(self, eng, fn, reads=(), writes=()):
        return self._add(eng, fn, reads, writes, False, None)

    def dma(self, eng, fn, sem, reads=(), writes=(), store=False):
        o = self._add(eng, fn, reads, writes, True, sem)
        if store:
            self.all_stores.append(o)
        return o

    def barrier(self, wait_dma=True):
        lasts = [self.last[e] for e in self.ENGS if self.last[e] is not None]
        pend = list(self.dma_pending) if wait_dma else []
        if wait_dma:
            self.dma_pending = []
        for e in self.ENGS:
            o = Op()
            o.eng = e
            o.fn = None
            o.is_dma = False
            o.needs_inc = False
            o.sem = None
            o.val = 0
            o.waits = []
            o.known = None
            o.idx = len(self.ops)
            o.hoist = False
            o.raw = set()
            o.oth = set(x for x in lasts if x.eng != e) | set(pend)
            self.ops.append(o)

    def final_wait(self, eng):
        o = Op()
        o.eng = eng
        o.fn = None
        o.is_dma = False
        o.needs_inc = False
        o.sem = None
        o.val = 0
        o.waits = []
        o.known = None
        o.idx = len(self.ops)
        o.hoist = False
        o.raw = set()
        o.oth = set(self.all_stores)
        self.ops.append(o)

    def resolve(self, esem):
        for o in self.ops:
            for d, is_raw in [(d, True) for d in o.raw] + [(d, False) for d in o.oth]:
                if d.is_dma:
                    continue
                if d.eng == o.eng and not o.is_dma:
                    if o.eng == "pe" or not is_raw:
                        continue
                d.needs_inc = True
        cnt = {e: 0 for e in self.ENGS}
        for o in self.ops:
            if o.fn is not None and not o.is_dma and o.needs_inc:
                cnt[o.eng] += 1
                o.sem = esem[o.eng]
                o.val = cnt[o.eng]
        seen = {e: {} for e in self.ENGS}
        nw = 0
        for o in self.ops:
            se = seen[o.eng]
            deps = []
            for d in o.raw:
                deps.append((d, True))
            for d in o.oth:
                deps.append((d, False))
            deps.sort(key=lambda t: t[0].idx)
            for d, is_raw in deps:
                if not d.is_dma:
                    if d.eng == o.eng and not o.is_dma:
                        if o.eng == "pe" or not is_raw:
                            continue
                key = id(d.sem)
                if se.get(key, (None, 0))[1] >= d.val:
                    continue
                o.waits.append((d.sem, d.val))
                nw += 1
                for k, (sm, v) in d.known.items():
                    if se.get(k, (None, 0))[1] < v:
                        se[k] = (sm, v)
            if o.fn is not None and (o.is_dma or o.needs_inc):
                kn = dict(se)
                kn[id(o.sem)] = (o.sem, o.val)
                o.known = kn
            else:
                o.known = dict(se) if o.fn is not None else None
        prev = {e: None for e in self.ENGS}
        for o in self.ops:
            if o.fn is None:
                continue
            if o.hoist and prev[o.eng] is not None and o.waits:
                prev[o.eng].waits = prev[o.eng].waits + o.waits
                o.waits = []
            prev[o.eng] = o
        for o in self.ops:
            if len(o.waits) > 1:
                best = {}
                for sm, v in o.waits:
                    k = id(sm)
                    if k not in best or best[k][1] < v:
                        best[k] = (sm, v)
                o.waits = list(best.values())
        return nw

    def emit(self, eng, e, esem):
        for o in self.ops:
            if o.eng != eng:
                continue
            for sm, v in o.waits:
                e.wait_ge(sm, v)
            if o.fn is None:
                continue
            ins = o.fn(e)
            if o.is_dma:
                ins.then_inc(o.sem, 16)
            elif o.needs_inc:
                ins.then_inc(esem[eng], 1)


class Bank:
    __slots__ = ("ap", "buf", "i")


class Psum:
    def __init__(self, aps):
        self.banks = []
        for i, ap in enumerate(aps):
            b = Bank()
            b.ap = ap
            b.buf = Buf("ps%d" % i)
            b.i = i
            self.banks.append(b)
        self.held = [False] * len(aps)
        self.ptr = {"all": 0, "o": 0, "g": 3}
        self.split = False

    def alloc(self, kind=None):
        if not self.split:
            rng, key = list(range(8)), "all"
        elif kind == "o":
            rng, key = [0, 1, 2], "o"
        else:
            rng, key = [3, 4, 5, 6, 7], "g"
        n = len(rng)
        start = rng.index(self.ptr[key]) if self.ptr[key] in rng else 0
        for k in range(n):
            i = rng[(start + k) % n]
            if not self.held[i]:
                self.held[i] = True
                self.ptr[key] = rng[(rng.index(i) + 1) % n]
                return self.banks[i]
        raise RuntimeError("PSUM exhausted (%s)" % key)

    def free(self, b):
        assert self.held[b.i]
        self.held[b.i] = False


def build_program(L=DEPTH, flags=("attn", "ffn"), dumps=()):
    nc = bass.Bass("TRN2", target_bir_lowering=False, dynamic_dma_scratch_size=4096)
    CM = _cols(L)
    NP = CM["_n"]

    def din(name, shape):
        return nc.dram_tensor(name, list(shape), F32, kind="ExternalInput").ap()

    x_d = din("x", (S, D))
    wq_d = din("wq", (L, 128, 8, 1024))
    wkv_d = din("wkv", (L, 128, 8, 672))
    wo_d = din("wo", (L, 128, 8, 1024))
    wuq_d = din("wuq", (L, 128, 2, 384))
    wukv_d = din("wukv", (L, 128, 512))
    w1_d = din("w1", (L, NG, 128, 8, 512))
    w2_d = din("w2", (L, NG, 128, 4, 1024))
    cbf_d = din("cbf", (128, CB_N))
    cf32_d = din("cf32", (128, 128 + NP))
    out_d = nc.dram_tensor("out", [S, D], F32, kind="ExternalOutput").ap()

    T = Tracker()
    ARENA_BYTES = 139776

    with ExitStack() as es:
        XT = es.enter_context(nc.sbuf_tensor("XT", [128, 8, S], F32))
        cbf = es.enter_context(nc.sbuf_tensor("cbf_sb", [128, CB_N], BF16))
        cf32 = es.enter_context(nc.sbuf_tensor("cf32_sb", [128, 128 + NP], F32))
        sinkE = es.enter_context(nc.sbuf_tensor("sinkE", [128, 6 * L], F32))
        arena = es.enter_context(nc.sbuf_tensor("arena", [128, ARENA_BYTES // 2], BF16))
        ps_tiles = [es.enter_context(nc.psum_tensor("psb%d" % i, [128, 512], F32)) for i in range(8)]
        esem = {e: es.enter_context(nc.semaphore("sem_" + e)) for e in ("pe", "act", "dve", "pool")}
        dsem_names = ["cbf", "cf32", "xs0", "xs1", "xs2", "xs3", "xs4", "xs5", "xs6", "xs7", "os0", "os1", "wq", "wkv", "wo", "wuq", "wukv",
                      "w1a", "w1b", "w2a", "w2b"]
        dsem = {n: es.enter_context(nc.semaphore("dsem_" + n)) for n in dsem_names}

        psum = Psum([t[:] for t in ps_tiles])

        def view(off, shape, dt):
            n = 1
            for s_ in shape[1:]:
                n *= s_
            nb = n * (4 if dt == F32 else 2)
            assert off % 4 == 0 and off + nb <= ARENA_BYTES
            ap = arena[:, off // 2: (off + nb) // 2]
            if dt == F32:
                ap = ap.bitcast(F32)
            if len(shape) == 3:
                ap = ap.rearrange("p (a b) -> p a b", a=shape[1])
            return ap

        off = [0]

        def carve(shape, dt):
            n = 1
            for s_ in shape[1:]:
                n *= s_
            nb = n * (4 if dt == F32 else 2)
            v = view(off[0], shape, dt)
            off[0] += nb
            return v

        sqh = [carve([128, 2, TGN], BF16), carve([128, 2, TGN], BF16)]
        rstd = carve([128, TGN], F32)
        BASE = off[0]
        assert BASE == 6144
        KA = carve([128, S], BF16)
        KB = carve([128, S], BF16)
        KC = carve([128, 4, S], BF16)
        Vall = carve([128, 16, 768], BF16)
        HM = [carve([128, 8, TGN], BF16), carve([128, 8, TGN], BF16)]
        chain = []
        for i in range(2):
            cd_ = dict(sqb=carve([128, TGN], BF16), rs=carve([128, TGN], F32), kn=carve([128, TGN], BF16),
                       t1=carve([128, TGN], F32),
                       b_sqb=Buf("sqb%d" % i), b_rs=Buf("rs%d" % i), b_kn=Buf("kn%d" % i), b_t1=Buf("t1%d" % i))
            cd_["t2"] = cd_["rs"]
            cd_["b_t2"] = cd_["b_rs"]
            chain.append(cd_)
        CQN_OFF = off[0]
        cqn = carve([128, 2, TGN], BF16)
        ckvn = view(CQN_OFF, [128, TGN], BF16)
        kr = view(CQN_OFF + 1024, [128, TGN], BF16)
        rc = [carve([128, TGN], F32)]
        assert off[0] == 88064, off[0]
        WKVQ_OFF = off[0]
        Wkv = view(WKVQ_OFF, [128, 8, 672], BF16)
        Qbuf0 = view(WKVQ_OFF, [128, 6, TGN], BF16)
        Qz = view(WKVQ_OFF + 6144, [128, 6, TGN], BF16)
        NPT = 4
        PT = [view(WKVQ_OFF + 12288 + i * 1024, [128, TGN], BF16) for i in range(NPT)]
        off[0] = WKVQ_OFF + 16384
        Wq = carve([128, 8, 1024], BF16)
        Wo = carve([128, 8, 1024], BF16)
        Wuq = carve([128, 2, 384], BF16)
        Wukv = carve([128, 512], BF16)
        assert off[0] == ARENA_BYTES, off[0]
        h2T = view(BASE, [128, 8, S], BF16)
        hidden = view(BASE + 32768, [128, 4, S], BF16)
        W1 = [view(BASE + 49152 + i * 8192, [128, 8, 512], BF16) for i in range(2)]
        W2 = [view(BASE + 65536 + i * 8192, [128, 4, 1024], BF16) for i in range(2)]
        assert BASE + 81920 <= WKVQ_OFF
        NXS = 8
        xstage = [view(BASE + 32768 + i * 4096, [128, 1024], F32) for i in range(NXS)]
        yT2 = [view(BASE, [128, 8, TGN], F32), view(BASE + 16384, [128, 8, TGN], F32)]
        ostage = [view(BASE + 32768 + i * 4096, [128, 1024], F32) for i in range(2)]

        ones128 = cbf[:, CB_ONES:CB_ONES + 128]
        blockones = cbf[:, CB_BLK:CB_BLK + 128]
        Rm = cbf[:, CB_RM:CB_RM + 128]
        maskA = cbf[:, CB_MASK:CB_MASK + 6 * 384].rearrange("p (h n) -> p h n", h=6)
        cosB = cbf[:, CB_COSB:CB_COSB + S]
        sinB = cbf[:, CB_SINB:CB_SINB + S]
        csC = cbf[:, CB_CSC:CB_CSC + S]
        ident = cf32[:, 0:128]
        smallp = cf32[:, 128:128 + NP]

        b_cbf = Buf("cbf")
        b_cf32 = Buf("cf32")
        b_sinkE = Buf("sinkE")
        b_XT = [[Buf("XT%d_%d" % (c, g)) for g in range(NTG)] for c in range(8)]
        b_HM = [[Buf("HM%d_%d" % (b, c)) for c in range(8)] for b in range(2)]
        b_sq = [Buf("sq0"), Buf("sq1")]
        b_rstd = Buf("rstd")
        b_KA = [Buf("KA%d" % g) for g in range(NTG)]
        b_KB = [Buf("KB%d" % g) for g in range(NTG)]
        b_KC = [[Buf("KC%d_%d" % (h, g)) for g in range(NTG)] for h in range(4)]
        b_V = [Buf("V%d" % g) for g in range(NTG)]
        b_Vones = Buf("Vones")
        b_cqn = Buf("cqn")
        b_kr = Buf("kr")
        b_rc = [Buf("rc0")]
        b_ckvn = Buf("ckvn")
        b_Q0 = [Buf("Q0_%d" % i) for i in range(6)]
        b_Qz = [Buf("Qz_%d" % i) for i in range(3)]
        b_PT = [Buf("PT%d" % i) for i in range(NPT)]
        b_Wkv = Buf("Wkv")
        b_Wq = Buf("Wq")
        b_Wo = Buf("Wo")
        b_Wuq = Buf("Wuq")
        b_Wukv = Buf("Wukv")
        b_W1 = [Buf("W1a"), Buf("W1b")]
        b_W2 = [Buf("W2a"), Buf("W2b")]
        b_h2T = [[Buf("h2T%d_%d" % (c, g)) for g in range(NTG)] for c in range(8)]
        b_hid = [[Buf("hid%d_%d" % (j, g)) for g in range(NTG)] for j in range(4)]
        b_xs = [Buf("xs%d" % i) for i in range(8)]
        b_yT2 = [[Buf("yT%d_%d" % (i, c)) for c in range(8)] for i in range(2)]
        b_os = [Buf("os0"), Buf("os1")]

        def mm(out, lhsT, rhs, start, stop, reads, writes, skip=False):
            if skip:
                fn = lambda e: e.matmul(out, lhsT=lhsT, rhs=rhs, start=start, stop=stop, skip_group_check=True)
            else:
                fn = lambda e: e.matmul(out, lhsT=lhsT, rhs=rhs, start=start, stop=stop)
            return T.op("pe", fn, reads, writes)

        def act(out, in_, func, reads, writes, scale=1.0, bias=None):
            if bias is None:
                return T.op("act", lambda e: e.activation(out=out, in_=in_, func=func, scale=scale), reads, writes)
            return T.op("act", lambda e: e.activation(out=out, in_=in_, func=func, bias=bias, scale=scale),
                        reads, writes)

        def tt(eng, out, in0, in1, op, reads, writes):
            return T.op(eng, lambda e: e.tensor_tensor(out=out, in0=in0, in1=in1, op=op), reads, writes)

        def stt(eng, out, in0, scalar, in1, op0, op1, reads, writes):
            return T.op(eng, lambda e: e.scalar_tensor_tensor(out=out, in0=in0, scalar=scalar, in1=in1,
                                                              op0=op0, op1=op1), reads, writes)

        def cp(eng, out, in_, reads, writes):
            if eng == "act":
                return T.op("act", lambda e: e.copy(out=out, in_=in_), reads, writes)
            return T.op(eng, lambda e: e.tensor_copy(out=out, in_=in_), reads, writes)

        def wload(key, dst, src, buf):
            return T.dma("pool", lambda e: e.dma_start(out=dst, in_=src), dsem[key], reads=(), writes=[buf])

        dump_outs = []

        def dump(name, ap, bufs):
            if name not in dumps:
                return
            shp = list(ap.shape)
            dt = ap.dtype
            dd = nc.dram_tensor("dbg_" + name, shp, dt, kind="ExternalOutput").ap()
            sm = es.enter_context(nc.semaphore("dsem_dbg_" + name))
            idx = tuple(slice(None) for _ in shp)
            T.dma("sp", lambda e: e.dma_start(out=dd[idx], in_=ap), sm, list(bufs), (), store=True)
            dump_outs.append("dbg_" + name)

        build_program.dump_outs = dump_outs

        def gen_norm(tg, gcol0, dst_views, dst_bufs):
            tok = slice(tg * TGN, (tg + 1) * TGN)
            bank = psum.alloc()
            for q in range(4):
                sv = sqh[q % 2]
                sb = b_sq[q % 2]
                for i in range(2):
                    c = 2 * q + i
                    act(sv[:, i, :], XT[:, c, tok], AF.Square, [b_XT[c][tg]], [sb])
                for i in range(2):
                    c = 2 * q + i
                    mm(bank.ap, ones128, sv[:, i, :], c == 0, c == 7, [sb, b_cbf], [bank.buf])
                yield
            act(rstd, bank.ap, AF.Ln, [bank.buf], [b_rstd], scale=1.0 / D, bias=EPS)
            psum.free(bank)
            act(rstd, rstd, AF.Exp, [b_rstd], [b_rstd], scale=-0.5)
            yield
            for c in range(8):
                eng = "dve"
                stt(eng, dst_views[c], XT[:, c, tok], smallp[:, gcol0 + c:gcol0 + c + 1], rstd,
                    ALU.mult, ALU.mult, [b_XT[c][tg], b_rstd, b_cf32], [dst_bufs[c]])
                if c % 2 == 1:
                    yield

        def emit_norm(tg, gcol0, dst_views, dst_bufs):
            for _ in gen_norm(tg, gcol0, dst_views, dst_bufs):
                pass

        def chain_front(pb, gcol, si):
            cs = chain[si]
            act(cs["sqb"], pb.ap, AF.Square, [pb.buf], [cs["b_sqb"]])
            ssb = psum.alloc()
            mm(ssb.ap, blockones, cs["sqb"], True, True, [cs["b_sqb"], b_cbf], [ssb.buf])
            act(cs["rs"], ssb.ap, AF.Ln, [ssb.buf], [cs["b_rs"]], scale=1.0 / 64, bias=EPS)
            psum.free(ssb)
            act(cs["rs"], cs["rs"], AF.Exp, [cs["b_rs"]], [cs["b_rs"]], scale=-0.5)
            stt("dve", cs["kn"], pb.ap, smallp[:, gcol:gcol + 1], cs["rs"], ALU.mult, ALU.mult,
                [pb.buf, cs["b_rs"], b_cf32], [cs["b_kn"]])
            psum.free(pb)

        def chain_back(si, tok, dst_ap, dst_buf, split=None):
            cs = chain[si]
            rot = psum.alloc()
            mm(rot.ap, Rm, cs["kn"], True, True, [cs["b_kn"], b_cbf], [rot.buf])
            tt("pool", cs["t1"], cs["kn"], cosB[:, tok], ALU.mult, [cs["b_kn"], b_cbf], [cs["b_t1"]])
            tt("dve", cs["t2"], rot.ap, sinB[:, tok], ALU.mult, [rot.buf, b_cbf], [cs["b_t2"]])
            psum.free(rot)
            if split is None:
                tt("pool", dst_ap, cs["t1"], cs["t2"], ALU.add, [cs["b_t1"], cs["b_t2"]], [dst_buf])
            else:
                d0, d1 = split
                tt("pool", d0, cs["t1"][0:64, :], cs["t2"][0:64, :], ALU.add, [cs["b_t1"], cs["b_t2"]], [dst_buf])
                tt("dve", d1, cs["t1"][64:128, :], cs["t2"][64:128, :], ALU.add, [cs["b_t1"], cs["b_t2"]], [dst_buf])

        def vaug(grp, tile_i, hf):
            c0 = grp * 192 + 64 * hf
            return Vall[:, tile_i, c0:c0 + 128]

        VALL_ELEM = (6144 + 4096 + 4096 + 16384) // 2
        PSTEP = ARENA_BYTES // 2

        def normalize(O, hf, dst_chunk_views, dst_buf, ri, sink_col=None):
            nr = slice(64 * hf, 64 * hf + 64)
            dr = slice(64 - 64 * hf, 128 - 64 * hf)
            r = rc[ri]
            rb = b_rc[ri]
            if sink_col is not None:
                act(r[dr, :], O.ap[dr, :], AF.Ln, [O.buf, b_sinkE], [rb], bias=sinkE[dr, sink_col:sink_col + 1])
                act(r[dr, :], r[dr, :], AF.Exp, [rb], [rb], scale=-1.0)
            else:
                T.op("dve", lambda e: e.reciprocal(out=r[dr, :], in_=O.ap[dr, :]), [O.buf], [rb])
            tt("dve", dst_chunk_views[nr, :], O.ap[nr, :], r[dr, :], ALU.mult, [O.buf, rb], [dst_buf])
            psum.free(O)

        ptc = [0]
        rcc = [0]

        def attend_dense(kfun, qap, kbufs, qbuf, vgrp, hf, scale, dst_view, dst_buf):
            O = psum.alloc()
            Sb = [None] * 16

            def issue_S(j):
                s_ = psum.alloc()
                mm(s_.ap, kfun(j), qap, True, True, [kbufs[j // 4], qbuf], [s_.buf])
                Sb[j] = s_

            base = ptc[0]
            ptc[0] += 16
            issue_S(0)
            issue_S(1)
            for j in range(16):
                pi = (base + j) % NPT
                act(PT[pi], Sb[j].ap, AF.Exp, [Sb[j].buf], [b_PT[pi]], scale=scale)
                psum.free(Sb[j])
                if j + 2 < 16:
                    issue_S(j + 2)
                mm(O.ap, vaug(vgrp, j, hf), PT[pi], j == 0, j == 15,
                   [b_V[j // 4], b_Vones, b_PT[pi]], [O.buf])
            normalize(O, hf, dst_view, dst_buf, 0)

        def idle(n):
            for _ in range(n):
                yield

        def par(*gens):
            gens = [g for g in gens if g is not None]
            while gens:
                alive = []
                for g in gens:
                    try:
                        next(g)
                        alive.append(g)
                    except StopIteration:
                        pass
                gens = alive
                if gens:
                    yield

        def seq(*gens):
            for g in gens:
                if g is not None:
                    yield from g

        def gen_norm_d(tg, gcol0, dst_views, dst_bufs):
            tok = slice(tg * TGN, (tg + 1) * TGN)
            bank = psum.alloc()

            def sq(q):
                xin = XT[:, 2 * q:2 * q + 2, tok]
                tt("pool", sqh[q % 2], xin, xin, ALU.mult, [b_XT[2 * q][tg], b_XT[2 * q + 1][tg]], [b_sq[q % 2]])

            def mms(q):
                for i in range(2):
                    c = 2 * q + i
                    mm(bank.ap, ones128, sqh[q % 2][:, i, :], c == 0, c == 7, [b_sq[q % 2], b_cbf], [bank.buf])

            sq(0)
            yield
            sq(1)
            yield from idle(3)
            mms(0)
            yield from idle(2)
            sq(2)
            yield
            mms(1)
            yield from idle(2)
            sq(3)
            yield from idle(2)
            mms(2)
            yield from idle(2)
            mms(3)
            yield from idle(3)
            act(rstd, bank.ap, AF.Ln, [bank.buf], [b_rstd], scale=1.0 / D, bias=EPS)
            psum.free(bank)
            act(rstd, rstd, AF.Exp, [b_rstd], [b_rstd], scale=-0.5)
            yield from idle(4)
            for c in range(8):
                stt("dve", dst_views[c], XT[:, c, tok], smallp[:, gcol0 + c:gcol0 + c + 1], rstd,
                    ALU.mult, ALU.mult, [b_XT[c][tg], b_rstd, b_cf32], [dst_bufs[c]])
                yield

        def run_side(side, n):
            if side is None:
                return
            for _ in range(n):
                try:
                    next(side)
                except StopIteration:
                    return

        def drain(side):
            if side is not None:
                for _ in side:
                    pass


        T.dma("pool", lambda e: e.dma_start(out=cbf[:], in_=cbf_d[:, :]), dsem["cbf"], (), [b_cbf])
        T.dma("sp", lambda e: e.dma_start(out=cf32[:], in_=cf32_d[:, :]), dsem["cf32"], (), [b_cf32])

        def prefetch_attn_weights(l, part=None):
            if part in (None, 0):
                wload("wkv", Wkv, wkv_d[l], b_Wkv)
                wload("wukv", Wukv, wukv_d[l], b_Wukv)
            if part in (None, 1):
                wload("wq", Wq, wq_d[l], b_Wq)
                wload("wuq", Wuq, wuq_d[l], b_Wuq)
                wload("wo", Wo, wo_d[l], b_Wo)

        prefetch_attn_weights(0, 0)
        for t_ in range(16):
            k = t_ % NXS
            rows = slice(t_ * 128, (t_ + 1) * 128)
            T.dma("sp", (lambda e, k=k, rows=rows: e.dma_start(out=xstage[k], in_=x_d[rows, :])),
                  dsem["xs%d" % k], (), [b_xs[k]])
            for half in range(2):
                bank = psum.alloc()
                for i in range(4):
                    c = half * 4 + i
                    T.op("pe", (lambda e, bank=bank, i=i, c=c, k=k:
                                e.transpose(bank.ap[:, i * 128:(i + 1) * 128], xstage[k][:, c * 128:(c + 1) * 128], ident)),
                         [b_xs[k], b_cf32], [bank.buf])
                dst = XT[:, half * 4:half * 4 + 4, t_ * 128:(t_ + 1) * 128]
                src = bank.ap.rearrange("p (a b) -> p a b", a=4)
                cp("dve", dst, src, [bank.buf], [b_XT[half * 4 + i][t_ // 4] for i in range(4)])
                psum.free(bank)
        act(sinkE[:], smallp[:, CM["sink"]:CM["sink"] + 6 * L], AF.Exp, [b_cf32], [b_sinkE])
        T.barrier(wait_dma=False)
        prefetch_attn_weights(0, 1)

        for l in range(L):
            g_attn = CM["attn"] + 8 * l
            g_mlp = CM["mlp"] + 8 * l
            g_bq = CM["bq"] + l
            g_bk = CM["bk"] + l
            g_cq = CM["cq"] + 2 * l
            g_ckv = CM["ckv"] + l

            for g in range(4 if "attn" in flags else 0):
                T.op("pool", (lambda e, g=g: e.memset(Vall[:, :, g * 192 + 64:g * 192 + 128], 1.0)), [], [b_Vones])

            for tg in range(NTG if "attn" in flags else 0):
                tok = slice(tg * TGN, (tg + 1) * TGN)
                b = tg % 2
                hT = HM[b]
                if tg == 0:
                    emit_norm(tg, g_attn, [hT[:, c, :] for c in range(8)], b_HM[b])
                ng = None

                def nstep(n):
                    run_side(ng, n)

                def projK(lo, hi, M):
                    bank = psum.alloc()
                    for kc in range(8):
                        mm(bank.ap[0:M, :], Wkv[:, kc, lo:hi], hT[:, kc, :], kc == 0, kc == 7,
                           [b_Wkv, b_HM[b][kc]], [bank.buf])
                    return bank

                def projV(t4):
                    bank = psum.alloc()
                    for kc in range(8):
                        mm(bank.ap[:, 0:256], hT[:, kc, t4 * 128:(t4 + 1) * 128], Wkv[:, kc, 416:672],
                           kc == 0, kc == 7, [b_Wkv, b_HM[b][kc]], [bank.buf])
                    return bank

                def evacV(bank, t4, g0):
                    dst = bass.AP(arena, VALL_ELEM + (tg * 4 + t4) * 768 + g0 * 192,
                                  [[PSTEP, 128], [192, 2], [128, 2], [1, 64]])
                    src = bank.ap[:, 0:256].rearrange("p (g k c) -> p g k c", g=2, k=2)
                    cp("dve", dst, src, [bank.buf], [b_V[tg]])
                    psum.free(bank)

                pka = projK(0, 128, 128)
                pkb = projK(128, 256, 128)
                pckv = projK(256, 384, 128)
                pkr = projK(320, 416, 96)
                pv0 = projV(0)
                pv1 = projV(1)
                cp("act", KA[:, tok], pka.ap, [pka.buf], [b_KA[tg]])
                psum.free(pka)
                cp("dve", kr[0:96, :], pkr.ap[0:96, :], [pkr.buf], [b_kr])
                psum.free(pkr)
                evacV(pv0, 0, 0)
                evacV(pv1, 1, 0)
                if tg + 1 < NTG:
                    nb_ = (tg + 1) % 2
                    ng = gen_norm_d(tg + 1, g_attn, [HM[nb_][:, c, :] for c in range(8)], b_HM[nb_])
                nstep(4)
                chain_front(pkb, g_bk, 0)
                nstep(3)
                cs1 = chain[1]
                act(cs1["sqb"], pckv.ap, AF.Square, [pckv.buf], [cs1["b_sqb"]])
                ss2 = psum.alloc()
                mm(ss2.ap, ones128, cs1["sqb"], True, True, [cs1["b_sqb"], b_cbf], [ss2.buf])
                rotk = psum.alloc()
                mm(rotk.ap[0:96, :], Rm[0:96, 0:96], kr[0:96, :], True, True, [b_kr, b_cbf], [rotk.buf])
                nstep(3)
                pv2 = projV(2)
                pv3 = projV(3)
                nstep(3)
                act(cs1["rs"], ss2.ap, AF.Ln, [ss2.buf], [cs1["b_rs"]], scale=1.0 / 128, bias=EPS)
                psum.free(ss2)
                act(cs1["rs"], cs1["rs"], AF.Exp, [cs1["b_rs"]], [cs1["b_rs"]], scale=-0.5)
                stt("dve", ckvn, pckv.ap, smallp[:, g_ckv:g_ckv + 1], cs1["rs"], ALU.mult, ALU.mult,
                    [pckv.buf, cs1["b_rs"], b_cf32], [b_ckvn])
                psum.free(pckv)
                nstep(3)
                tt("pool", cs1["t1"][64:96, :], kr[64:96, :], csC[64:96, tok], ALU.mult, [b_kr, b_cbf], [cs1["b_t1"]])
                tt("dve", cs1["t2"][64:96, :], rotk.ap[64:96, :], csC[0:32, tok], ALU.mult, [rotk.buf, b_cbf], [cs1["b_t2"]])
                psum.free(rotk)
                tt("dve", KC[64:96, 0, tok], cs1["t1"][64:96, :], cs1["t2"][64:96, :], ALU.add,
                   [cs1["b_t1"], cs1["b_t2"]], [b_KC[0][tg]])
                for h in range(1, 4):
                    cp("act", KC[64:96, h, tok], KC[64:96, 0, tok], [b_KC[0][tg]], [b_KC[h][tg]])
                nstep(3)
                evacV(pv2, 2, 0)
                evacV(pv3, 3, 0)
                nstep(3)
                chain_back(0, tok, KB[:, tok], b_KB[tg])
                nstep(3)
                for hp in range(2):
                    bank = psum.alloc()
                    mm(bank.ap, Wukv[:, hp * 128:(hp + 1) * 128], ckvn, True, True, [b_Wukv, b_ckvn], [bank.buf])
                    cp("dve", KC[0:64, 2 * hp, tok], bank.ap[0:64, :], [bank.buf], [b_KC[2 * hp][tg]])
                    cp("act", KC[0:64, 2 * hp + 1, tok], bank.ap[64:128, :], [bank.buf], [b_KC[2 * hp + 1][tg]])
                    psum.free(bank)
                    nstep(3)
                for t4 in range(4):
                    bank = psum.alloc()
                    mm(bank.ap[:, 0:256], ckvn[:, t4 * 128:(t4 + 1) * 128], Wukv[:, 256:512], True, True,
                       [b_Wukv, b_ckvn], [bank.buf])
                    evacV(bank, t4, 2)
                    nstep(2)
                drain(ng)

            T.barrier()
            if "attn" in flags:
                T.op("pool", lambda e: e.memset(Qz[:], 0.0), [], b_Qz)

            def q_norm(tg):
                b = tg % 2
                emit_norm(tg, g_attn, [HM[b][:, c, :] for c in range(8)], b_HM[b])

            def w_out(tg, fs):
                tok = slice(tg * TGN, (tg + 1) * TGN)
                b = tg % 2
                mixed = HM[b]
                for f in fs:
                    bank = psum.alloc()
                    for mc in range(8):
                        mm(bank.ap, Wo[:, mc, f * 128:(f + 1) * 128], mixed[:, mc, :], mc == 0, mc == 7,
                           [b_Wo, b_HM[b][mc]], [bank.buf])
                    tt("dve", XT[:, f, tok], XT[:, f, tok], bank.ap, ALU.add, [b_XT[f][tg], bank.buf], [b_XT[f][tg]])
                    psum.free(bank)

            def _projQ(tg, ci):
                b = tg % 2
                hT = HM[b]
                bank = psum.alloc()
                for kc in range(8):
                    mm(bank.ap, Wq[:, kc, ci * 128:(ci + 1) * 128], hT[:, kc, :], kc == 0, kc == 7,
                       [b_Wq, b_HM[b][kc]], [bank.buf])
                return bank

            def gen_proj(bank, tg, ci):
                b = tg % 2
                hT = HM[b]
                for kc in range(8):
                    mm(bank.ap, Wq[:, kc, ci * 128:(ci + 1) * 128], hT[:, kc, :], kc == 0, kc == 7,
                       [b_Wq, b_HM[b][kc]], [bank.buf])
                    if kc == 3:
                        yield
                yield

            def gen_qa(tg, c, state=None, key=None):
                while state is not None and not state[key]:
                    yield
                bank = psum.alloc()
                yield from gen_proj(bank, tg, c)
                T.op("pool", (lambda e, c=c: e.memset(Qbuf0[64:128, 2 * c, :], 0.0)), [], [b_Q0[2 * c]])
                T.op("pool", (lambda e, c=c: e.memset(Qbuf0[0:64, 2 * c + 1, :], 0.0)), [], [b_Q0[2 * c + 1]])
                yield from idle(2)
                cp("dve", Qbuf0[0:64, 2 * c, :], bank.ap[0:64, :], [bank.buf], [b_Q0[2 * c]])
                cp("dve", Qbuf0[64:128, 2 * c + 1, :], bank.ap[64:128, :], [bank.buf], [b_Q0[2 * c + 1]])
                psum.free(bank)
                yield

            def front_QA(tg, cs=(0, 1, 2)):
                for c in cs:
                    bank = _projQ(tg, c)
                    T.op("pool", (lambda e, c=c: e.memset(Qbuf0[64:128, 2 * c, :], 0.0)), [], [b_Q0[2 * c]])
                    T.op("pool", (lambda e, c=c: e.memset(Qbuf0[0:64, 2 * c + 1, :], 0.0)), [], [b_Q0[2 * c + 1]])
                    cp("dve", Qbuf0[0:64, 2 * c, :], bank.ap[0:64, :], [bank.buf], [b_Q0[2 * c]])
                    cp("dve", Qbuf0[64:128, 2 * c + 1, :], bank.ap[64:128, :], [bank.buf], [b_Q0[2 * c + 1]])
                    psum.free(bank)

            def gen_wout(prev):
                tok = slice(prev * TGN, (prev + 1) * TGN)
                b = prev % 2
                mixed = HM[b]
                for f in range(8):
                    bank = psum.alloc()
                    for mc in range(8):
                        mm(bank.ap, Wo[:, mc, f * 128:(f + 1) * 128], mixed[:, mc, :], mc == 0, mc == 7,
                           [b_Wo, b_HM[b][mc]], [bank.buf])
                        if mc == 3:
                            yield
                    yield from idle(3)
                    tt("dve", XT[:, f, tok], XT[:, f, tok], bank.ap, ALU.add, [b_XT[f][prev], bank.buf], [b_XT[f][prev]])
                    psum.free(bank)
                    yield

            def gen_qb_chain(tg, ci, si, state):
                tok = slice(tg * TGN, (tg + 1) * TGN)
                cs = chain[si]
                p = psum.alloc()
                yield from gen_proj(p, tg, 3 + ci)
                yield from idle(3)
                act(cs["sqb"], p.ap, AF.Square, [p.buf], [cs["b_sqb"]])
                yield from idle(3)
                ssb = psum.alloc()
                mm(ssb.ap, blockones, cs["sqb"], True, True, [cs["b_sqb"], b_cbf], [ssb.buf])
                yield from idle(3)
                act(cs["rs"], ssb.ap, AF.Ln, [ssb.buf], [cs["b_rs"]], scale=1.0 / 64, bias=EPS)
                psum.free(ssb)
                act(cs["rs"], cs["rs"], AF.Exp, [cs["b_rs"]], [cs["b_rs"]], scale=-0.5)
                yield from idle(4)
                stt("dve", cs["kn"], p.ap, smallp[:, g_bq:g_bq + 1], cs["rs"], ALU.mult, ALU.mult,
                    [p.buf, cs["b_rs"], b_cf32], [cs["b_kn"]])
                psum.free(p)
                yield from idle(3)
                rot = psum.alloc()
                mm(rot.ap, Rm, cs["kn"], True, True, [cs["b_kn"], b_cbf], [rot.buf])
                tt("pool", cs["t1"], cs["kn"], cosB[:, tok], ALU.mult, [cs["b_kn"], b_cbf], [cs["b_t1"]])
                yield from idle(3)
                tt("dve", cs["t2"], rot.ap, sinB[:, tok], ALU.mult, [rot.buf, b_cbf], [cs["b_t2"]])
                psum.free(rot)
                yield from idle(3)
                while not state["b_done"]:
                    yield
                tt("pool", Qz[0:64, 2 * ci, :], cs["t1"][0:64, :], cs["t2"][0:64, :], ALU.add,
                   [cs["b_t1"], cs["b_t2"]], [b_Qz[ci]])
                tt("dve", Qz[64:128, 2 * ci + 1, :], cs["t1"][64:128, :], cs["t2"][64:128, :], ALU.add,
                   [cs["b_t1"], cs["b_t2"]], [b_Qz[ci]])
                yield

            def gen_cq_chain(tg):
                pcq = []
                for c in range(2):
                    bk = psum.alloc()
                    pcq.append(bk)
                    yield from gen_proj(bk, tg, 6 + c)
                yield from idle(3)
                cqsq = sqh[0]
                act(cqsq[:, 0, :], pcq[0].ap, AF.Square, [pcq[0].buf], [b_sq[0]])
                act(cqsq[:, 1, :], pcq[1].ap, AF.Square, [pcq[1].buf], [b_sq[0]])
                yield from idle(4)
                sscq = psum.alloc()
                mm(sscq.ap, ones128, cqsq[:, 0, :], True, False, [b_sq[0], b_cbf], [sscq.buf])
                mm(sscq.ap, ones128, cqsq[:, 1, :], False, True, [b_sq[0], b_cbf], [sscq.buf])
                yield from idle(3)
                cs1 = chain[1]
                act(cs1["rs"], sscq.ap, AF.Ln, [sscq.buf], [cs1["b_rs"]], scale=1.0 / 256, bias=EPS)
                psum.free(sscq)
                act(cs1["rs"], cs1["rs"], AF.Exp, [cs1["b_rs"]], [cs1["b_rs"]], scale=-0.5)
                yield from idle(4)
                for c in range(2):
                    stt("dve", cqn[:, c, :], pcq[c].ap, smallp[:, g_cq + c:g_cq + c + 1], cs1["rs"],
                        ALU.mult, ALU.mult, [pcq[c].buf, cs1["b_rs"], b_cf32], [b_cqn])
                    psum.free(pcq[c])
                    yield

            def gen_cup(tg, hs):
                tok = slice(tg * TGN, (tg + 1) * TGN)
                banks = {}
                for h in hs:
                    bank = psum.alloc()
                    for c in range(2):
                        mm(bank.ap[0:96, :], Wuq[:, c, h * 96:(h + 1) * 96], cqn[:, c, :], c == 0, c == 1,
                           [b_Wuq, b_cqn], [bank.buf])
                    banks[h] = bank
                    yield
                yield from idle(2)
                for h in hs:
                    cp("dve", Qbuf0[0:96, h, :], banks[h].ap[0:96, :], [banks[h].buf], [b_Q0[h]])
                    psum.free(banks[h])
                    yield
                yield from idle(2)
                rqs = {}
                for h in hs:
                    cs = chain[h % 2]
                    rq = psum.alloc()
                    mm(rq.ap[0:96, :], Rm[0:96, 0:96], Qbuf0[0:96, h, :], True, True, [b_Q0[h], b_cbf], [rq.buf])
                    tt("pool", cs["t1"][64:96, :], Qbuf0[64:96, h, :], csC[64:96, tok], ALU.mult,
                       [b_Q0[h], b_cbf], [cs["b_t1"]])
                    rqs[h] = rq
                    yield
                yield from idle(2)
                for h in hs:
                    cs = chain[h % 2]
                    tt("dve", cs["t2"][64:96, :], rqs[h].ap[64:96, :], csC[0:32, tok], ALU.mult,
                       [rqs[h].buf, b_cbf], [cs["b_t2"]])
                    psum.free(rqs[h])
                    yield
                yield from idle(2)
                for h in hs:
                    cs = chain[h % 2]
                    tt("dve", Qbuf0[64:96, h, :], cs["t1"][64:96, :], cs["t2"][64:96, :], ALU.add,
                       [cs["b_t1"], cs["b_t2"]], [b_Q0[h]])
                    yield

            def gen_side(tg, state):
                s1 = gen_cup(tg, (0, 1))
                if tg + 1 < NTG:
                    nb = (tg + 1) % 2
                    s2 = par(gen_norm_d(tg + 1, g_attn, [HM[nb][:, c, :] for c in range(8)], b_HM[nb]),
                             gen_cup(tg, (2, 3)))
                    s3 = seq(gen_qb_chain(tg + 1, 0, 0, state), gen_qb_chain(tg + 1, 1, 1, state))
                    s4 = seq(gen_qb_chain(tg + 1, 2, 0, state), gen_cq_chain(tg + 1))
                    s5 = seq(gen_qa(tg + 1, 2), gen_qa(tg + 1, 0, state, "c01_done"))
                    return seq(s1, s2, s3, s4, s5)
                return seq(s1, gen_cup(tg, (2, 3)))

            def attn_A(tg, side=None):
                b = tg % 2
                mixed = HM[b]
                tiles = []
                for c in range(3):
                    for hf in range(2):
                        for qb in range(4):
                            tiles.append((c, hf, qb))
                NT = len(tiles)
                Sb = {}
                pis = {}
                Ob = {}

                def a_S(t):
                    c, hf, qb = tiles[t]
                    pr = slice(64 * hf, 64 * hf + 64)
                    n = 4 * tg + qb
                    jjs = [jj for jj in range(3) if 0 <= n - 1 + jj <= 15]
                    s_ = psum.alloc()
                    for jj in jjs:
                        kt = n - 1 + jj
                        mm(s_.ap[:, jj * 128:(jj + 1) * 128], KA[:, kt * 128:(kt + 1) * 128],
                           Qbuf0[:, 2 * c + hf, qb * 128:(qb + 1) * 128], True, True,
                           [b_KA[kt // 4], b_Q0[2 * c + hf]], [s_.buf])
                    Sb[t] = (s_, jjs, n)

                def a_exp(t):
                    c, hf, qb = tiles[t]
                    hA = c + 3 * hf
                    s_, jjs, n = Sb[t]
                    lo, hi = jjs[0] * 128, (jjs[-1] + 1) * 128
                    pi = ptc[0] % NPT
                    ptc[0] += 1
                    pis[t] = pi
                    act(PT[pi][:, lo:hi], s_.ap[:, lo:hi], AF.Exp, [s_.buf], [b_PT[pi]], scale=SC_AB)
                    psum.free(s_)
                    tt("dve", PT[pi][:, lo:hi], PT[pi][:, lo:hi], maskA[:, hA, lo:hi], ALU.mult,
                       [b_PT[pi], b_cbf], [b_PT[pi]])

                def a_pv(t):
                    c, hf, qb = tiles[t]
                    hA = c + 3 * hf
                    s_, jjs, n = Sb[t]
                    pi = pis[t]
                    if qb == 0:
                        Ob[(c, hf)] = psum.alloc("o")
                    O = Ob[(c, hf)]
                    for jj in jjs:
                        kt = n - 1 + jj
                        mm(O.ap[:, qb * 128:(qb + 1) * 128], vaug(0, kt, hf),
                           PT[pi][:, jj * 128:(jj + 1) * 128], (qb == 0 and jj == jjs[0]), jj == jjs[-1],
                           [b_V[kt // 4], b_Vones, b_PT[pi]], [O.buf], skip=True)
                    if qb == 3:
                        pend.append((t + 3, O, hf, c, hA))

                pend = []

                def a_norm(t):
                    while pend and pend[0][0] <= t:
                        _, O, hf, c, hA = pend.pop(0)
                        normalize(O, hf, mixed[:, c, :], b_HM[b][c], 0, sink_col=6 * l + hA)

                a_S(0)
                a_S(1)
                a_S(2)
                a_exp(0)
                a_exp(1)
                for t in range(NT):
                    if t + 3 < NT:
                        a_S(t + 3)
                    if t + 2 < NT:
                        a_exp(t + 2)
                    a_pv(t)
                    a_norm(t)
                    run_side(side, 2)
                a_norm(10 ** 9)
                drain(side)

            def gen_mid(tg, state):
                tok = slice(tg * TGN, (tg + 1) * TGN)
                for h in range(4):
                    bank = psum.alloc()
                    for c in range(2):
                        mm(bank.ap[0:96, :], Wuq[:, c, h * 96:(h + 1) * 96], cqn[:, c, :], c == 0, c == 1,
                           [b_Wuq, b_cqn], [bank.buf])
                    cp("dve", Qbuf0[0:96, h, :], bank.ap[0:96, :], [bank.buf], [b_Q0[h]])
                    psum.free(bank)
                    yield
                for h in range(4):
                    si = h % 2
                    cs = chain[si]
                    rq = psum.alloc()
                    mm(rq.ap[0:96, :], Rm[0:96, 0:96], Qbuf0[0:96, h, :], True, True, [b_Q0[h], b_cbf], [rq.buf])
                    tt("pool", cs["t1"][64:96, :], Qbuf0[64:96, h, :], csC[64:96, tok], ALU.mult,
                       [b_Q0[h], b_cbf], [cs["b_t1"]])
                    tt("dve", cs["t2"][64:96, :], rq.ap[64:96, :], csC[0:32, tok], ALU.mult, [rq.buf, b_cbf], [cs["b_t2"]])
                    psum.free(rq)
                    yield
                    tt("dve", Qbuf0[64:96, h, :], cs["t1"][64:96, :], cs["t2"][64:96, :], ALU.add,
                       [cs["b_t1"], cs["b_t2"]], [b_Q0[h]])
                    yield
                if tg + 1 < NTG:
                    yield from gen_norm(tg + 1, g_attn, [HM[(tg + 1) % 2][:, c, :] for c in range(8)], b_HM[(tg + 1) % 2])
                    yield from gen_front_rest(tg + 1, state)

            def attn_BC(tg, side=None, state=None):
                b = tg % 2
                mixed = HM[b]
                heads = []
                for c in range(3):
                    for hf in range(2):
                        heads.append(dict(kfun=(lambda j: KB[:, j * 128:(j + 1) * 128]), qap=Qz[:, 2 * c + hf, :],
                                          kbufs=b_KB, qbuf=b_Qz[c], vgrp=1, hf=hf, scale=SC_AB,
                                          dst=mixed[:, 3 + c, :], dbuf=b_HM[b][3 + c]))
                for h in range(4):
                    heads.append(dict(kfun=(lambda j, h=h: KC[0:96, h, j * 128:(j + 1) * 128]), qap=Qbuf0[0:96, h, :],
                                      kbufs=b_KC[h], qbuf=b_Q0[h], vgrp=2 + h // 2, hf=h % 2, scale=SC_C,
                                      dst=mixed[:, 6 + h // 2, :], dbuf=b_HM[b][6 + h // 2]))
                tiles = [(hi, j) for hi in range(len(heads)) for j in range(16)]
                NT = len(tiles)
                Sb = {}
                Ob = {}

                def issue_S(t):
                    hi, j = tiles[t]
                    h = heads[hi]
                    s_ = psum.alloc()
                    mm(s_.ap, h["kfun"](j), h["qap"], True, True, [h["kbufs"][j // 4], h["qbuf"]], [s_.buf])
                    Sb[t] = s_

                issue_S(0)
                issue_S(1)
                for t in range(NT):
                    hi, j = tiles[t]
                    h = heads[hi]
                    pi = ptc[0] % NPT
                    ptc[0] += 1
                    act(PT[pi], Sb[t].ap, AF.Exp, [Sb[t].buf], [b_PT[pi]], scale=h["scale"])
                    psum.free(Sb[t])
                    if t + 2 < NT:
                        issue_S(t + 2)
                    if j == 0:
                        Ob[hi] = psum.alloc("o")
                    O = Ob[hi]
                    pvop = mm(O.ap, vaug(h["vgrp"], j, h["hf"]), PT[pi], j == 0, j == 15,
                              [b_V[j // 4], b_Vones, b_PT[pi]], [O.buf])
                    pvop.hoist = False
                    if j == 15:
                        normalize(O, h["hf"], h["dst"], h["dbuf"], 0)
                    if t >= 94 and state is not None:
                        state["b_done"] = True
                    run_side(side, 1)
                if state is not None:
                    state["b_done"] = True
                    state["c01_done"] = True
                drain(side)

            if "attn" in flags:
                psum.split = True
                q_norm(0)
                st0 = {"b_done": True}
                drain(seq(gen_qb_chain(0, 0, 0, st0), gen_qb_chain(0, 1, 1, st0), gen_qb_chain(0, 2, 0, st0),
                          gen_cq_chain(0)))
                for tg in range(NTG):
                    front_QA(tg, (0, 1, 2) if tg == 0 else (1,))
                    attn_A(tg, gen_wout(tg - 1) if tg > 0 else None)
                    st = {"b_done": False, "c01_done": False}
                    attn_BC(tg, gen_side(tg, st), st)
                w_out(NTG - 1, range(8))
                psum.split = False

            T.barrier()

            def load_ffn(G):
                i = G % 2
                wload("w1" + "ab"[i], W1[i], w1_d[l, G], b_W1[i])
                wload("w2" + "ab"[i], W2[i], w2_d[l, G], b_W2[i])

            NGX = NG if "ffn" in flags else 0
            if NGX:
                load_ffn(0)
                load_ffn(1)
            if l + 1 < L and not NGX:
                prefetch_attn_weights(l + 1)
            for tg in range(NTG if NGX else 0):
                tok = slice(tg * TGN, (tg + 1) * TGN)
                emit_norm(tg, g_mlp, [h2T[:, c, tok] for c in range(8)], [b_h2T[c][tg] for c in range(8)])
            if NGX:
                dump("h2T", h2T, [b_h2T[c][g] for c in range(8) for g in range(NTG)])
            for G in range(NGX):
                i = G % 2
                for tg in range(NTG):
                    tok = slice(tg * TGN, (tg + 1) * TGN)
                    for j in range(4):
                        bank = psum.alloc()
                        for kc in range(8):
                            mm(bank.ap, W1[i][:, kc, j * 128:(j + 1) * 128], h2T[:, kc, tok], kc == 0, kc == 7,
                               [b_W1[i], b_h2T[kc][tg]], [bank.buf])
                        act(hidden[:, j, tok], bank.ap, AF.Relu, [bank.buf], [b_hid[j][tg]])
                        psum.free(bank)
                        tt("pool", hidden[:, j, tok], hidden[:, j, tok], hidden[:, j, tok], ALU.mult,
                           [b_hid[j][tg]], [b_hid[j][tg]])
                for tg in range(NTG):
                    tok = slice(tg * TGN, (tg + 1) * TGN)
                    for f in range(8):
                        bank = psum.alloc()
                        for j in range(4):
                            mm(bank.ap, W2[i][:, j, f * 128:(f + 1) * 128], hidden[:, j, tok], j == 0, j == 3,
                               [b_W2[i], b_hid[j][tg]], [bank.buf])
                        tt("dve", XT[:, f, tok], XT[:, f, tok], bank.ap, ALU.add, [b_XT[f][tg], bank.buf], [b_XT[f][tg]])
                        psum.free(bank)
                if G == 0:
                    dump("hidden0", hidden, [b_hid[j][g] for j in range(4) for g in range(NTG)])
                    dump("W1_0", W1[0], [b_W1[0]])
                    dump("W2_0", W2[0], [b_W2[0]])
                    dump("XT_G0", XT[:], [b_XT[c][g] for c in range(8) for g in range(NTG)])
                if G + 2 < NG:
                    load_ffn(G + 2)
                if G == 1 and l + 1 < L:
                    prefetch_attn_weights(l + 1, 0)
                if G == 3 and l + 1 < L:
                    prefetch_attn_weights(l + 1, 1)

            T.barrier()

        g_fin = CM["final"]
        oc = 0
        emit_norm(0, g_fin, [yT2[0][:, c, :] for c in range(8)], b_yT2[0])
        for tg in range(NTG):
            yT = yT2[tg % 2]
            b_yT = b_yT2[tg % 2]
            if tg + 1 < NTG:
                emit_norm(tg + 1, g_fin, [yT2[(tg + 1) % 2][:, c, :] for c in range(8)], b_yT2[(tg + 1) % 2])
            for t4 in range(4):
                k = oc % 2
                oc += 1
                for half in range(2):
                    bank = psum.alloc()
                    for i in range(4):
                        c = half * 4 + i
                        T.op("pe", (lambda e, bank=bank, i=i, c=c, t4=t4, yT=yT:
                                    e.transpose(bank.ap[:, i * 128:(i + 1) * 128], yT[:, c, t4 * 128:(t4 + 1) * 128], ident)),
                             [b_yT[c], b_cf32], [bank.buf])
                    cp("act" if half == 0 else "dve", ostage[k][:, half * 512:(half + 1) * 512], bank.ap,
                       [bank.buf], [b_os[k]])
                    psum.free(bank)
                rows = slice(tg * TGN + t4 * 128, tg * TGN + (t4 + 1) * 128)
                T.dma("sp", (lambda e, k=k, rows=rows: e.dma_start(out=out_d[rows, :], in_=ostage[k])),
                      dsem["os%d" % k], [b_os[k]], (), store=True)
        T.final_wait("sp")

        nw = T.resolve(esem)
        build_program.stats = dict(n_ops=len(T.ops), n_waits=nw)

        with nc.Block() as block:
            @block.tensor
            def _(e):
                T.emit("pe", e, esem)

            @block.scalar
            def _(e):
                T.emit("act", e, esem)

            @block.vector
            def _(e):
                T.emit("dve", e, esem)

            @block.gpsimd
            def _(e):
                T.emit("pool", e, esem)

            @block.sync
            def _(e):
                T.emit("sp", e, esem)
    return nc


def _const_tables():
    cb = np.zeros((128, CB_N), np.float32)
    cb[:, CB_ONES:CB_ONES + 128] = 1.0
    for h in range(2):
        cb[64 * h:64 * h + 64, CB_BLK + 64 * h:CB_BLK + 64 * h + 64] = 1.0
    for blk in range(4):
        o = 32 * blk
        for m in range(16):
            cb[o + m + 16, CB_RM + o + m] = -1.0
            cb[o + m + 16 - 16, CB_RM + o + m + 16] = 1.0
    slopes = 2.0 ** (-8.0 * np.arange(1, 7, dtype=np.float64) / 6)
    k = np.arange(128)[:, None]
    for h in range(6):
        for jj in range(3):
            q = np.arange(128)[None, :]
            dist = np.abs(q - k + (1 - jj) * 128).astype(np.float64)
            m = np.where(dist <= 128, np.exp(-slopes[h] * dist), 0.0)
            cb[:, CB_MASK + h * 384 + jj * 128:CB_MASK + h * 384 + (jj + 1) * 128] = m
    def tables(pos, dim):
        inv = (10000.0 ** (-np.arange(0, dim, 2, dtype=np.float32) / np.float32(dim))).astype(np.float32)
        ang = pos.astype(np.float32)[:, None] * inv[None, :]
        ang = np.concatenate([ang, ang], axis=-1)
        return np.cos(ang).astype(np.float32), np.sin(ang).astype(np.float32)
    t = np.arange(S)
    cr, sr = tables(t // 64, 32)
    cc, sc = tables(t % 64, 32)
    cm, sm = tables(t, 32)
    cosb = np.concatenate([cr, cc], axis=-1).T
    sinb = np.concatenate([sr, sc], axis=-1).T
    cb[0:64, CB_COSB:CB_COSB + S] = cosb
    cb[64:128, CB_COSB:CB_COSB + S] = cosb
    cb[0:64, CB_SINB:CB_SINB + S] = sinb
    cb[64:128, CB_SINB:CB_SINB + S] = sinb
    cb[64:96, CB_CSC:CB_CSC + S] = cm.T
    cb[0:32, CB_CSC:CB_CSC + S] = sm.T
    return cb


def _prep_weights(L, attn_norm, w_in, a_sink, b_q_norm, b_k_norm, c_q_norm, c_kv_norm,
                  w_uq, w_ukv, w_out, mlp_norm, w_ff1, w_ff2, final_norm):
    f = lambda a: np.asarray(a, dtype=np.float32)
    w_in, w_out, w_uq, w_ukv, w_ff1, w_ff2 = map(f, (w_in, w_out, w_uq, w_ukv, w_ff1, w_ff2))
    pair = []
    for c in range(3):
        pair += list(range(c * 64, c * 64 + 64)) + list(range((c + 3) * 64, (c + 3) * 64 + 64))
    pair = np.array(pair)
    colsQ = np.concatenate([pair, 640 + pair, np.arange(1280, 1536)])
    colsKV = np.concatenate([np.arange(384, 512), np.arange(1024, 1152), np.arange(1536, 1664),
                             np.arange(1664, 1696), np.arange(512, 640), np.arange(1152, 1280)])
    rowsO = np.concatenate([pair, 384 + pair, np.arange(768, 1024)])
    wq = np.ascontiguousarray(w_in[:L][:, :, colsQ].reshape(L, 8, 128, 1024).transpose(0, 2, 1, 3))
    wkv = np.ascontiguousarray(w_in[:L][:, :, colsKV].reshape(L, 8, 128, 672).transpose(0, 2, 1, 3))
    wo = np.ascontiguousarray(w_out[:L][:, rowsO, :].reshape(L, 8, 128, 1024).transpose(0, 2, 1, 3))
    wuq = np.ascontiguousarray(w_uq[:L].reshape(L, 2, 128, 384).transpose(0, 2, 1, 3))
    t_ = w_ukv[:L].reshape(L, 128, 4, 128)
    wukv = np.ascontiguousarray(np.concatenate([t_[:, :, :, 0:64].reshape(L, 128, 256),
                                                t_[:, :, :, 64:128].reshape(L, 128, 256)], axis=-1))
    w1 = np.ascontiguousarray(w_ff1[:L].reshape(L, 8, 128, NG, 512).transpose(0, 3, 2, 1, 4))
    w2 = np.ascontiguousarray(w_ff2[:L].reshape(L, NG, 4, 128, 1024).transpose(0, 1, 3, 2, 4))
    CM = _cols(L)
    sp = np.zeros((128, CM["_n"]), np.float32)
    an = f(attn_norm)[:L].reshape(L, 8, 128)
    mn = f(mlp_norm)[:L].reshape(L, 8, 128)
    for l in range(L):
        sp[:, CM["attn"] + 8 * l:CM["attn"] + 8 * l + 8] = an[l].T
        sp[:, CM["mlp"] + 8 * l:CM["mlp"] + 8 * l + 8] = mn[l].T
        sp[:, CM["bq"] + l] = np.tile(f(b_q_norm)[l], 2)
        sp[:, CM["bk"] + l] = np.tile(f(b_k_norm)[l], 2)
        sp[:, CM["cq"] + 2 * l:CM["cq"] + 2 * l + 2] = f(c_q_norm)[l].reshape(2, 128).T
        sp[:, CM["ckv"] + l] = f(c_kv_norm)[l]
        sp[:, CM["sink"] + 6 * l:CM["sink"] + 6 * l + 6] = np.broadcast_to(f(a_sink)[l][None, :], (128, 6))
    sp[:, CM["final"]:CM["final"] + 8] = f(final_norm).reshape(8, 128).T
    cf32 = np.concatenate([np.eye(128, dtype=np.float32), sp], axis=1)
    return dict(wq=wq, wkv=wkv, wo=wo, wuq=wuq, wukv=wukv, w1=w1, w2=w2,
                cbf=_const_tables(), cf32=np.ascontiguousarray(cf32))


_NC_CACHE = {}


def run(x, L=DEPTH, flags=("attn", "ffn"), dumps=(), **params):
    x = np.asarray(x, dtype=np.float32)
    shared = _prep_weights(L, **params)
    key = (L, tuple(flags), tuple(dumps))
    if key not in _NC_CACHE:
        _NC_CACHE[key] = build_program(L, flags, dumps)
    nc = _NC_CACHE[key]
    in_maps = []
    for i in range(NCORES):
        m = dict(shared)
        m["x"] = np.ascontiguousarray(x[i])
        in_maps.append(m)
    res = run_bass_kernel_spmd(nc, in_maps, core_ids=list(range(NCORES)))
    run.last_results = res.results
    return np.stack([np.asarray(r["out"], dtype=np.float32) for r in res.results], axis=0)


def kernel(x, attn_norm, w_in, a_sink, b_q_norm, b_k_norm, c_q_norm, c_kv_norm,
           w_uq, w_ukv, w_out, mlp_norm, w_ff1, w_ff2, final_norm):
    return run(x, DEPTH, attn_norm=attn_norm, w_in=w_in, a_sink=a_sink, b_q_norm=b_q_norm,
               b_k_norm=b_k_norm, c_q_norm=c_q_norm, c_kv_norm=c_kv_norm, w_uq=w_uq, w_ukv=w_ukv,
               w_out=w_out, mlp_norm=mlp_norm, w_ff1=w_ff1, w_ff2=w_ff2, final_norm=final_norm)
```

```python
import math
from contextlib import ExitStack

import numpy as np
import concourse.bass as bass
import concourse.mybir as mybir
from concourse.bass_utils import run_bass_kernel_spmd

F32 = mybir.dt.float32
BF16 = mybir.dt.bfloat16
AF = mybir.ActivationFunctionType
ALU = mybir.AluOpType

S = 2048
D = 1024
DEPTH = 4
NCORES = 8
EPS = 1e-6
TGN = 512
NTG = S // TGN
D_FF = 4096
NG = 8
SC_AB = 0.125
SC_C = 96.0 ** -0.5

def _cols(L):
    m = {}
    o = 0
    for name, n in (("attn", 8 * L), ("mlp", 8 * L), ("final", 8), ("bq", L), ("bk", L),
                    ("cq", 2 * L), ("ckv", L), ("sink", 6 * L)):
        m[name] = o
        o += n
    m["_n"] = o
    return m

CB_ONES = 0
CB_BLK = 128
CB_RM = 256
CB_MASK = 384
CB_COSB = CB_MASK + 6 * 384
CB_SINB = CB_COSB + S
CB_CSC = CB_SINB + S
CB_N = CB_CSC + S


class Buf:
    __slots__ = ("name", "w", "r")

    def __init__(self, name):
        self.name = name
        self.w = None
        self.r = []


class Op:
    __slots__ = ("eng", "fn", "raw", "oth", "needs_inc", "sem", "val", "waits", "is_dma", "known", "idx", "hoist")


class Tracker:
    ENGS = ("pe", "act", "dve", "pool", "sp")

    def __init__(self):
        self.ops = []
        self.last = {e: None for e in self.ENGS}
        self.dma_count = {}
        self.dma_pending = []
        self.all_stores = []

    def _add(self, eng, fn, reads, writes, is_dma, sem):
        o = Op()
        o.eng = eng
        o.fn = fn
        o.is_dma = is_dma
        o.needs_inc = False
        o.sem = sem
        o.val = 0
        o.waits = []
        o.known = None
        o.idx = len(self.ops)
        o.hoist = False
        raw = set()
        oth = set()
        for b in reads:
            if b.w is not None:
                raw.add(b.w)
        for b in writes:
            if b.w is not None:
                oth.add(b.w)
            for r in b.r:
                oth.add(r)
        oth -= raw
        raw.discard(o)
        oth.discard(o)
        o.raw = raw
        o.oth = oth
        for b in reads:
            b.r.append(o)
        for b in writes:
            b.w = o
            b.r = []
        self.ops.append(o)
        if is_dma:
            n = self.dma_count.get(sem, 0) + 1
            self.dma_count[sem] = n
            o.val = 16 * n
            self.dma_pending.append(o)
        else:
            self.last[eng] = o
        return o

    def op(self, eng, fn, reads=(), writes=()):
        return self._add(eng, fn, reads, writes, False, None)

    def dma(self, eng, fn, sem, reads=(), writes=(), store=False):
        o = self._add(eng, fn, reads, writes, True, sem)
        if store:
            self.all_stores.append(o)
        return o

    def barrier(self, wait_dma=True):
        lasts = [self.last[e] for e in self.ENGS if self.last[e] is not None]
        pend = list(self.dma_pending) if wait_dma else []
        if wait_dma:
            self.dma_pending = []
        for e in self.ENGS:
            o = Op()
            o.eng = e
            o.fn = None
            o.is_dma = False
            o.needs_inc = False
            o.sem = None
            o.val = 0
            o.waits = []
            o.known = None
            o.idx = len(self.ops)
            o.hoist = False
            o.raw = set()
            o.oth = set(x for x in lasts if x.eng != e) | set(pend)
            self.ops.append(o)

    def final_wait(self, eng):
        o = Op()
        o.eng = eng
        o.fn = None
        o.is_dma = False
        o.needs_inc = False
        o.sem = None
        o.val = 0
        o.waits = []
        o.known = None
        o.idx = len(self.ops)
        o.hoist = False
        o.raw = set()
        o.oth = set(self.all_stores)
        self.ops.append(o)

    def resolve(self, esem):
        for o in self.ops:
            for d, is_raw in [(d, True) for d in o.raw] + [(d, False) for d in o.oth]:
                if d.is_dma:
                    continue
                if d.eng == o.eng and not o.is_dma:
                    if o.eng == "pe" or not is_raw:
                        continue
                d.needs_inc = True
        cnt = {e: 0 for e in self.ENGS}
        for o in self.ops:
            if o.fn is not None and not o.is_dma and o.needs_inc:
                cnt[o.eng] += 1
                o.sem = esem[o.eng]
                o.val = cnt[o.eng]
        seen = {e: {} for e in self.ENGS}
        nw = 0
        for o in self.ops:
            se = seen[o.eng]
            deps = []
            for d in o.raw:
                deps.append((d, True))
            for d in o.oth:
                deps.append((d, False))
            deps.sort(key=lambda t: t[0].idx)
            for d, is_raw in deps:
                if not d.is_dma:
                    if d.eng == o.eng and not o.is_dma:
                        if o.eng == "pe" or not is_raw:
                            continue
                key = id(d.sem)
                if se.get(key, (None, 0))[1] >= d.val:
                    continue
                o.waits.append((d.sem, d.val))
                nw += 1
                for k, (sm, v) in d.known.items():
                    if se.get(k, (None, 0))[1] < v:
                        se[k] = (sm, v)
            if o.fn is not None and (o.is_dma or o.needs_inc):
                kn = dict(se)
                kn[id(o.sem)] = (o.sem, o.val)
                o.known = kn
            else:
                o.known = dict(se) if o.fn is not None else None
        prev = {e: None for e in self.ENGS}
        for o in self.ops:
            if o.fn is None:
                continue
            if o.hoist and prev[o.eng] is not None and o.waits:
                prev[o.eng].waits = prev[o.eng].waits + o.waits
                o.waits = []
            prev[o.eng] = o
        for o in self.ops:
            if len(o.waits) > 1:
                best = {}
                for sm, v in o.waits:
                    k = id(sm)
                    if k not in best or best[k][1] < v:
                        best[k] = (sm, v)
                o.waits = list(best.values())
        return nw

    def emit(self, eng, e, esem):
        for o in self.ops:
            if o.eng != eng:
                continue
            for sm, v in o.waits:
                e.wait_ge(sm, v)
            if o.fn is None:
                continue
            ins = o.fn(e)
            if o.is_dma:
                ins.then_inc(o.sem, 16)
            elif o.needs_inc:
                ins.then_inc(esem[eng], 1)


class Bank:
    __slots__ = ("ap", "buf", "i")


class Psum:
    def __init__(self, aps):
        self.banks = []
        for i, ap in enumerate(aps):
            b = Bank()
            b.ap = ap
            b.buf = Buf("ps%d" % i)
            b.i = i
            self.banks.append(b)
        self.held = [False] * len(aps)
        self.ptr = {"all": 0, "o": 0, "g": 3}
        self.split = False

    def alloc(self, kind=None):
        if not self.split:
            rng, key = list(range(8)), "all"
        elif kind == "o":
            rng, key = [0, 1, 2], "o"
        else:
            rng, key = [3, 4, 5, 6, 7], "g"
        n = len(rng)
        start = rng.index(self.ptr[key]) if self.ptr[key] in rng else 0
        for k in range(n):
            i = rng[(start + k) % n]
            if not self.held[i]:
                self.held[i] = True
                self.ptr[key] = rng[(rng.index(i) + 1) % n]
                return self.banks[i]
        raise RuntimeError("PSUM exhausted (%s)" % key)

    def free(self, b):
        assert self.held[b.i]
        self.held[b.i] = False


def build_program(L=DEPTH, flags=("attn", "ffn"), dumps=()):
    nc = bass.Bass("TRN2", target_bir_lowering=False, dynamic_dma_scratch_size=4096)
    CM = _cols(L)
    NP = CM["_n"]

    def din(name, shape):
        return nc.dram_tensor(name, list(shape), F32, kind="ExternalInput").ap()

    x_d = din("x", (S, D))
    wq_d = din("wq", (L, 128, 8, 1024))
    wkv_d = din("wkv", (L, 128, 8, 672))
    wo_d = din("wo", (L, 128, 8, 1024))
    wuq_d = din("wuq", (L, 128, 2, 384))
    wukv_d = din("wukv", (L, 128, 512))
    w1_d = din("w1", (L, NG, 128, 8, 512))
    w2_d = din("w2", (L, NG, 128, 4, 1024))
    cbf_d = din("cbf", (128, CB_N))
    cf32_d = din("cf32", (128, 128 + NP))
    out_d = nc.dram_tensor("out", [S, D], F32, kind="ExternalOutput").ap()

    T = Tracker()
    ARENA_BYTES = 139776

    with ExitStack() as es:
        XT = es.enter_context(nc.sbuf_tensor("XT", [128, 8, S], F32))
        cbf = es.enter_context(nc.sbuf_tensor("cbf_sb", [128, CB_N], BF16))
        cf32 = es.enter_context(nc.sbuf_tensor("cf32_sb", [128, 128 + NP], F32))
        sinkE = es.enter_context(nc.sbuf_tensor("sinkE", [128, 6 * L], F32))
        arena = es.enter_context(nc.sbuf_tensor("arena", [128, ARENA_BYTES // 2], BF16))
        ps_tiles = [es.enter_context(nc.psum_tensor("psb%d" % i, [128, 512], F32)) for i in range(8)]
        esem = {e: es.enter_context(nc.semaphore("sem_" + e)) for e in ("pe", "act", "dve", "pool")}
        dsem_names = ["cbf", "cf32", "xs0", "xs1", "xs2", "xs3", "xs4", "xs5", "xs6", "xs7", "os0", "os1", "wq", "wkv", "wo", "wuq", "wukv",
                      "w1a", "w1b", "w2a", "w2b"]
        dsem = {n: es.enter_context(nc.semaphore("dsem_" + n)) for n in dsem_names}

        psum = Psum([t[:] for t in ps_tiles])

        def view(off, shape, dt):
            n = 1
            for s_ in shape[1:]:
                n *= s_
            nb = n * (4 if dt == F32 else 2)
            assert off % 4 == 0 and off + nb <= ARENA_BYTES
            ap = arena[:, off // 2: (off + nb) // 2]
            if dt == F32:
                ap = ap.bitcast(F32)
            if len(shape) == 3:
                ap = ap.rearrange("p (a b) -> p a b", a=shape[1])
            return ap

        off = [0]

        def carve(shape, dt):
            n = 1
            for s_ in shape[1:]:
                n *= s_
            nb = n * (4 if dt == F32 else 2)
            v = view(off[0], shape, dt)
            off[0] += nb
            return v

        sqh = [carve([128, 2, TGN], BF16), carve([128, 2, TGN], BF16)]
        rstd = carve([128, TGN], F32)
        BASE = off[0]
        assert BASE == 6144
        KA = carve([128, S], BF16)
        KB = carve([128, S], BF16)
        KC = carve([128, 4, S], BF16)
        Vall = carve([128, 16, 768], BF16)
        HM = [carve([128, 8, TGN], BF16), carve([128, 8, TGN], BF16)]
        chain = []
        for i in range(2):
            cd_ = dict(sqb=carve([128, TGN], BF16), rs=carve([128, TGN], F32), kn=carve([128, TGN], BF16),
                       t1=carve([128, TGN], F32),
                       b_sqb=Buf("sqb%d" % i), b_rs=Buf("rs%d" % i), b_kn=Buf("kn%d" % i), b_t1=Buf("t1%d" % i))
            cd_["t2"] = cd_["rs"]
            cd_["b_t2"] = cd_["b_rs"]
            chain.append(cd_)
        CQN_OFF = off[0]
        cqn = carve([128, 2, TGN], BF16)
        ckvn = view(CQN_OFF, [128, TGN], BF16)
        kr = view(CQN_OFF + 1024, [128, TGN], BF16)
        rc = [carve([128, TGN], F32)]
        assert off[0] == 88064, off[0]
        WKVQ_OFF = off[0]
        Wkv = view(WKVQ_OFF, [128, 8, 672], BF16)
        Qbuf0 = view(WKVQ_OFF, [128, 6, TGN], BF16)
        Qz = view(WKVQ_OFF + 6144, [128, 6, TGN], BF16)
        NPT = 4
        PT = [view(WKVQ_OFF + 12288 + i * 1024, [128, TGN], BF16) for i in range(NPT)]
        off[0] = WKVQ_OFF + 16384
        Wq = carve([128, 8, 1024], BF16)
        Wo = carve([128, 8, 1024], BF16)
        Wuq = carve([128, 2, 384], BF16)
        Wukv = carve([128, 512], BF16)
        assert off[0] == ARENA_BYTES, off[0]
        h2T = view(BASE, [128, 8, S], BF16)
        hidden = view(BASE + 32768, [128, 4, S], BF16)
        W1 = [view(BASE + 49152 + i * 8192, [128, 8, 512], BF16) for i in range(2)]
        W2 = [view(BASE + 65536 + i * 8192, [128, 4, 1024], BF16) for i in range(2)]
        assert BASE + 81920 <= WKVQ_OFF
        NXS = 8
        xstage = [view(BASE + 32768 + i * 4096, [128, 1024], F32) for i in range(NXS)]
        yT2 = [view(BASE, [128, 8, TGN], F32), view(BASE + 16384, [128, 8, TGN], F32)]
        ostage = [view(BASE + 32768 + i * 4096, [128, 1024], F32) for i in range(2)]

        ones128 = cbf[:, CB_ONES:CB_ONES + 128]
        blockones = cbf[:, CB_BLK:CB_BLK + 128]
        Rm = cbf[:, CB_RM:CB_RM + 128]
        maskA = cbf[:, CB_MASK:CB_MASK + 6 * 384].rearrange("p (h n) -> p h n", h=6)
        cosB = cbf[:, CB_COSB:CB_COSB + S]
        sinB = cbf[:, CB_SINB:CB_SINB + S]
        csC = cbf[:, CB_CSC:CB_CSC + S]
        ident = cf32[:, 0:128]
        smallp = cf32[:, 128:128 + NP]

        b_cbf = Buf("cbf")
        b_cf32 = Buf("cf32")
        b_sinkE = Buf("sinkE")
        b_XT = [[Buf("XT%d_%d" % (c, g)) for g in range(NTG)] for c in range(8)]
        b_HM = [[Buf("HM%d_%d" % (b, c)) for c in range(8)] for b in range(2)]
        b_sq = [Buf("sq0"), Buf("sq1")]
        b_rstd = Buf("rstd")
        b_KA = [Buf("KA%d" % g) for g in range(NTG)]
        b_KB = [Buf("KB%d" % g) for g in range(NTG)]
        b_KC = [[Buf("KC%d_%d" % (h, g)) for g in range(NTG)] for h in range(4)]
        b_V = [Buf("V%d" % g) for g in range(NTG)]
        b_Vones = Buf("Vones")
        b_cqn = Buf("cqn")
        b_kr = Buf("kr")
        b_rc = [Buf("rc0")]
        b_ckvn = Buf("ckvn")
        b_Q0 = [Buf("Q0_%d" % i) for i in range(6)]
        b_Qz = [Buf("Qz_%d" % i) for i in range(3)]
        b_PT = [Buf("PT%d" % i) for i in range(NPT)]
        b_Wkv = Buf("Wkv")
        b_Wq = Buf("Wq")
        b_Wo = Buf("Wo")
        b_Wuq = Buf("Wuq")
        b_Wukv = Buf("Wukv")
        b_W1 = [Buf("W1a"), Buf("W1b")]
        b_W2 = [Buf("W2a"), Buf("W2b")]
        b_h2T = [[Buf("h2T%d_%d" % (c, g)) for g in range(NTG)] for c in range(8)]
        b_hid = [[Buf("hid%d_%d" % (j, g)) for g in range(NTG)] for j in range(4)]
        b_xs = [Buf("xs%d" % i) for i in range(8)]
        b_yT2 = [[Buf("yT%d_%d" % (i, c)) for c in range(8)] for i in range(2)]
        b_os = [Buf("os0"), Buf("os1")]

        def mm(out, lhsT, rhs, start, stop, reads, writes, skip=False):
            if skip:
                fn = lambda e: e.matmul(out, lhsT=lhsT, rhs=rhs, start=start, stop=stop, skip_group_check=True)
            else:
                fn = lambda e: e.matmul(out, lhsT=lhsT, rhs=rhs, start=start, stop=stop)
            return T.op("pe", fn, reads, writes)

        def act(out, in_, func, reads, writes, scale=1.0, bias=None):
            if bias is None:
                return T.op("act", lambda e: e.activation(out=out, in_=in_, func=func, scale=scale), reads, writes)
            return T.op("act", lambda e: e.activation(out=out, in_=in_, func=func, bias=bias, scale=scale),
                        reads, writes)

        def tt(eng, out, in0, in1, op, reads, writes):
            return T.op(eng, lambda e: e.tensor_tensor(out=out, in0=in0, in1=in1, op=op), reads, writes)

        def stt(eng, out, in0, scalar, in1, op0, op1, reads, writes):
            return T.op(eng, lambda e: e.scalar_tensor_tensor(out=out, in0=in0, scalar=scalar, in1=in1,
                                                              op0=op0, op1=op1), reads, writes)

        def cp(eng, out, in_, reads, writes):
            if eng == "act":
                return T.op("act", lambda e: e.copy(out=out, in_=in_), reads, writes)
            return T.op(eng, lambda e: e.tensor_copy(out=out, in_=in_), reads, writes)

        def wload(key, dst, src, buf):
            return T.dma("pool", lambda e: e.dma_start(out=dst, in_=src), dsem[key], reads=(), writes=[buf])

        dump_outs = []

        def dump(name, ap, bufs):
            if name not in dumps:
                return
            shp = list(ap.shape)
            dt = ap.dtype
            dd = nc.dram_tensor("dbg_" + name, shp, dt, kind="ExternalOutput").ap()
            sm = es.enter_context(nc.semaphore("dsem_dbg_" + name))
            idx = tuple(slice(None) for _ in shp)
            T.dma("sp", lambda e: e.dma_start(out=dd[idx], in_=ap), sm, list(bufs), (), store=True)
            dump_outs.append("dbg_" + name)

        build_program.dump_outs = dump_outs

        def gen_norm(tg, gcol0, dst_views, dst_bufs):
            tok = slice(tg * TGN, (tg + 1) * TGN)
            bank = psum.alloc()
            for q in range(4):
                sv = sqh[q % 2]
                sb = b_sq[q % 2]
                for i in range(2):
                    c = 2 * q + i
                    act(sv[:, i, :], XT[:, c, tok], AF.Square, [b_XT[c][tg]], [sb])
                for i in range(2):
                    c = 2 * q + i
                    mm(bank.ap, ones128, sv[:, i, :], c == 0, c == 7, [sb, b_cbf], [bank.buf])
                yield
            act(rstd, bank.ap, AF.Ln, [bank.buf], [b_rstd], scale=1.0 / D, bias=EPS)
            psum.free(bank)
            act(rstd, rstd, AF.Exp, [b_rstd], [b_rstd], scale=-0.5)
            yield
            for c in range(8):
                eng = "dve"
                stt(eng, dst_views[c], XT[:, c, tok], smallp[:, gcol0 + c:gcol0 + c + 1], rstd,
                    ALU.mult, ALU.mult, [b_XT[c][tg], b_rstd, b_cf32], [dst_bufs[c]])
                if c % 2 == 1:
                    yield

        def emit_norm(tg, gcol0, dst_views, dst_bufs):
            for _ in gen_norm(tg, gcol0, dst_views, dst_bufs):
                pass

        def chain_front(pb, gcol, si):
            cs = chain[si]
            act(cs["sqb"], pb.ap, AF.Square, [pb.buf], [cs["b_sqb"]])
            ssb = psum.alloc()
            mm(ssb.ap, blockones, cs["sqb"], True, True, [cs["b_sqb"], b_cbf], [ssb.buf])
            act(cs["rs"], ssb.ap, AF.Ln, [ssb.buf], [cs["b_rs"]], scale=1.0 / 64, bias=EPS)
            psum.free(ssb)
            act(cs["rs"], cs["rs"], AF.Exp, [cs["b_rs"]], [cs["b_rs"]], scale=-0.5)
            stt("dve", cs["kn"], pb.ap, smallp[:, gcol:gcol + 1], cs["rs"], ALU.mult, ALU.mult,
                [pb.buf, cs["b_rs"], b_cf32], [cs["b_kn"]])
            psum.free(pb)

        def chain_back(si, tok, dst_ap, dst_buf, split=None):
            cs = chain[si]
            rot = psum.alloc()
            mm(rot.ap, Rm, cs["kn"], True, True, [cs["b_kn"], b_cbf], [rot.buf])
            tt("pool", cs["t1"], cs["kn"], cosB[:, tok], ALU.mult, [cs["b_kn"], b_cbf], [cs["b_t1"]])
            tt("dve", cs["t2"], rot.ap, sinB[:, tok], ALU.mult, [rot.buf, b_cbf], [cs["b_t2"]])
            psum.free(rot)
            if split is None:
                tt("pool", dst_ap, cs["t1"], cs["t2"], ALU.add, [cs["b_t1"], cs["b_t2"]], [dst_buf])
            else:
                d0, d1 = split
                tt("pool", d0, cs["t1"][0:64, :], cs["t2"][0:64, :], ALU.add, [cs["b_t1"], cs["b_t2"]], [dst_buf])
                tt("dve", d1, cs["t1"][64:128, :], cs["t2"][64:128, :], ALU.add, [cs["b_t1"], cs["b_t2"]], [dst_buf])

        def vaug(grp, tile_i, hf):
            c0 = grp * 192 + 64 * hf
            return Vall[:, tile_i, c0:c0 + 128]

        VALL_ELEM = (6144 + 4096 + 4096 + 16384) // 2
        PSTEP = ARENA_BYTES // 2

        def normalize(O, hf, dst_chunk_views, dst_buf, ri, sink_col=None):
            nr = slice(64 * hf, 64 * hf + 64)
            dr = slice(64 - 64 * hf, 128 - 64 * hf)
            r = rc[ri]
            rb = b_rc[ri]
            if sink_col is not None:
                act(r[dr, :], O.ap[dr, :], AF.Ln, [O.buf, b_sinkE], [rb], bias=sinkE[dr, sink_col:sink_col + 1])
                act(r[dr, :], r[dr, :], AF.Exp, [rb], [rb], scale=-1.0)
            else:
                T.op("dve", lambda e: e.reciprocal(out=r[dr, :], in_=O.ap[dr, :]), [O.buf], [rb])
            tt("dve", dst_chunk_views[nr, :], O.ap[nr, :], r[dr, :], ALU.mult, [O.buf, rb], [dst_buf])
            psum.free(O)

        ptc = [0]
        rcc = [0]

        def attend_dense(kfun, qap, kbufs, qbuf, vgrp, hf, scale, dst_view, dst_buf):
            O = psum.alloc()
            Sb = [None] * 16

            def issue_S(j):
                s_ = psum.alloc()
                mm(s_.ap, kfun(j), qap, True, True, [kbufs[j // 4], qbuf], [s_.buf])
                Sb[j] = s_

            base = ptc[0]
            ptc[0] += 16
            issue_S(0)
            issue_S(1)
            for j in range(16):
                pi = (base + j) % NPT
                act(PT[pi], Sb[j].ap, AF.Exp, [Sb[j].buf], [b_PT[pi]], scale=scale)
                psum.free(Sb[j])
                if j + 2 < 16:
                    issue_S(j + 2)
                mm(O.ap, vaug(vgrp, j, hf), PT[pi], j == 0, j == 15,
                   [b_V[j // 4], b_Vones, b_PT[pi]], [O.buf])
            normalize(O, hf, dst_view, dst_buf, 0)

        def idle(n):
            for _ in range(n):
                yield

        def par(*gens):
            gens = [g for g in gens if g is not None]
            while gens:
                alive = []
                for g in gens:
                    try:
                        next(g)
                        alive.append(g)
                    except StopIteration:
                        pass
                gens = alive
                if gens:
                    yield

        def seq(*gens):
            for g in gens:
                if g is not None:
                    yield from g

        def gen_norm_d(tg, gcol0, dst_views, dst_bufs):
            tok = slice(tg * TGN, (tg + 1) * TGN)
            bank = psum.alloc()

            def sq(q):
                for i in range(2):
                    c = 2 * q + i
                    act(sqh[q % 2][:, i, :], XT[:, c, tok], AF.Square, [b_XT[c][tg]], [b_sq[q % 2]])

            def mms(q):
                for i in range(2):
                    c = 2 * q + i
                    mm(bank.ap, ones128, sqh[q % 2][:, i, :], c == 0, c == 7, [b_sq[q % 2], b_cbf], [bank.buf])

            sq(0)
            yield
            sq(1)
            yield from idle(3)
            mms(0)
            yield from idle(2)
            sq(2)
            yield
            mms(1)
            yield from idle(2)
            sq(3)
            yield from idle(2)
            mms(2)
            yield from idle(2)
            mms(3)
            yield from idle(3)
            act(rstd, bank.ap, AF.Ln, [bank.buf], [b_rstd], scale=1.0 / D, bias=EPS)
            psum.free(bank)
            act(rstd, rstd, AF.Exp, [b_rstd], [b_rstd], scale=-0.5)
            yield from idle(4)
            for c in range(8):
                stt("dve", dst_views[c], XT[:, c, tok], smallp[:, gcol0 + c:gcol0 + c + 1], rstd,
                    ALU.mult, ALU.mult, [b_XT[c][tg], b_rstd, b_cf32], [dst_bufs[c]])
                yield

        def run_side(side, n):
            if side is None:
                return
            for _ in range(n):
                try:
                    next(side)
                except StopIteration:
                    return

        def drain(side):
            if side is not None:
                for _ in side:
                    pass


        T.dma("pool", lambda e: e.dma_start(out=cbf[:], in_=cbf_d[:, :]), dsem["cbf"], (), [b_cbf])
        T.dma("sp", lambda e: e.dma_start(out=cf32[:], in_=cf32_d[:, :]), dsem["cf32"], (), [b_cf32])

        def prefetch_attn_weights(l, part=None):
            if part in (None, 0):
                wload("wkv", Wkv, wkv_d[l], b_Wkv)
                wload("wukv", Wukv, wukv_d[l], b_Wukv)
            if part in (None, 1):
                wload("wq", Wq, wq_d[l], b_Wq)
                wload("wuq", Wuq, wuq_d[l], b_Wuq)
                wload("wo", Wo, wo_d[l], b_Wo)

        prefetch_attn_weights(0, 0)
        for t_ in range(16):
            k = t_ % NXS
            rows = slice(t_ * 128, (t_ + 1) * 128)
            T.dma("sp", (lambda e, k=k, rows=rows: e.dma_start(out=xstage[k], in_=x_d[rows, :])),
                  dsem["xs%d" % k], (), [b_xs[k]])
            for half in range(2):
                bank = psum.alloc()
                for i in range(4):
                    c = half * 4 + i
                    T.op("pe", (lambda e, bank=bank, i=i, c=c, k=k:
                                e.transpose(bank.ap[:, i * 128:(i + 1) * 128], xstage[k][:, c * 128:(c + 1) * 128], ident)),
                         [b_xs[k], b_cf32], [bank.buf])
                dst = XT[:, half * 4:half * 4 + 4, t_ * 128:(t_ + 1) * 128]
                src = bank.ap.rearrange("p (a b) -> p a b", a=4)
                cp("dve", dst, src, [bank.buf], [b_XT[half * 4 + i][t_ // 4] for i in range(4)])
                psum.free(bank)
        act(sinkE[:], smallp[:, CM["sink"]:CM["sink"] + 6 * L], AF.Exp, [b_cf32], [b_sinkE])
        T.barrier(wait_dma=False)
        prefetch_attn_weights(0, 1)

        for l in range(L):
            g_attn = CM["attn"] + 8 * l
            g_mlp = CM["mlp"] + 8 * l
            g_bq = CM["bq"] + l
            g_bk = CM["bk"] + l
            g_cq = CM["cq"] + 2 * l
            g_ckv = CM["ckv"] + l

            for g in range(4 if "attn" in flags else 0):
                T.op("pool", (lambda e, g=g: e.memset(Vall[:, :, g * 192 + 64:g * 192 + 128], 1.0)), [], [b_Vones])

            for tg in range(NTG if "attn" in flags else 0):
                tok = slice(tg * TGN, (tg + 1) * TGN)
                b = tg % 2
                hT = HM[b]
                if tg == 0:
                    emit_norm(tg, g_attn, [hT[:, c, :] for c in range(8)], b_HM[b])
                ng = None

                def nstep(n):
                    run_side(ng, n)

                def projK(lo, hi, M):
                    bank = psum.alloc()
                    for kc in range(8):
                        mm(bank.ap[0:M, :], Wkv[:, kc, lo:hi], hT[:, kc, :], kc == 0, kc == 7,
                           [b_Wkv, b_HM[b][kc]], [bank.buf])
                    return bank

                def projV(t4):
                    bank = psum.alloc()
                    for kc in range(8):
                        mm(bank.ap[:, 0:256], hT[:, kc, t4 * 128:(t4 + 1) * 128], Wkv[:, kc, 416:672],
                           kc == 0, kc == 7, [b_Wkv, b_HM[b][kc]], [bank.buf])
                    return bank

                def evacV(bank, t4, g0):
                    dst = bass.AP(arena, VALL_ELEM + (tg * 4 + t4) * 768 + g0 * 192,
                                  [[PSTEP, 128], [192, 2], [128, 2], [1, 64]])
                    src = bank.ap[:, 0:256].rearrange("p (g k c) -> p g k c", g=2, k=2)
                    cp("dve", dst, src, [bank.buf], [b_V[tg]])
                    psum.free(bank)

                pka = projK(0, 128, 128)
                pkb = projK(128, 256, 128)
                pckv = projK(256, 384, 128)
                pkr = projK(320, 416, 96)
                pv0 = projV(0)
                pv1 = projV(1)
                cp("act", KA[:, tok], pka.ap, [pka.buf], [b_KA[tg]])
                psum.free(pka)
                cp("dve", kr[0:96, :], pkr.ap[0:96, :], [pkr.buf], [b_kr])
                psum.free(pkr)
                evacV(pv0, 0, 0)
                evacV(pv1, 1, 0)
                if tg + 1 < NTG:
                    nb_ = (tg + 1) % 2
                    ng = gen_norm_d(tg + 1, g_attn, [HM[nb_][:, c, :] for c in range(8)], b_HM[nb_])
                nstep(4)
                chain_front(pkb, g_bk, 0)
                nstep(3)
                cs1 = chain[1]
                act(cs1["sqb"], pckv.ap, AF.Square, [pckv.buf], [cs1["b_sqb"]])
                ss2 = psum.alloc()
                mm(ss2.ap, ones128, cs1["sqb"], True, True, [cs1["b_sqb"], b_cbf], [ss2.buf])
                rotk = psum.alloc()
                mm(rotk.ap[0:96, :], Rm[0:96, 0:96], kr[0:96, :], True, True, [b_kr, b_cbf], [rotk.buf])
                nstep(3)
                pv2 = projV(2)
                pv3 = projV(3)
                nstep(3)
                act(cs1["rs"], ss2.ap, AF.Ln, [ss2.buf], [cs1["b_rs"]], scale=1.0 / 128, bias=EPS)
                psum.free(ss2)
                act(cs1["rs"], cs1["rs"], AF.Exp, [cs1["b_rs"]], [cs1["b_rs"]], scale=-0.5)
                stt("dve", ckvn, pckv.ap, smallp[:, g_ckv:g_ckv + 1], cs1["rs"], ALU.mult, ALU.mult,
                    [pckv.buf, cs1["b_rs"], b_cf32], [b_ckvn])
                psum.free(pckv)
                nstep(3)
                tt("pool", cs1["t1"][64:96, :], kr[64:96, :], csC[64:96, tok], ALU.mult, [b_kr, b_cbf], [cs1["b_t1"]])
                tt("dve", cs1["t2"][64:96, :], rotk.ap[64:96, :], csC[0:32, tok], ALU.mult, [rotk.buf, b_cbf], [cs1["b_t2"]])
                psum.free(rotk)
                tt("dve", KC[64:96, 0, tok], cs1["t1"][64:96, :], cs1["t2"][64:96, :], ALU.add,
                   [cs1["b_t1"], cs1["b_t2"]], [b_KC[0][tg]])
                for h in range(1, 4):
                    cp("act", KC[64:96, h, tok], KC[64:96, 0, tok], [b_KC[0][tg]], [b_KC[h][tg]])
                nstep(3)
                evacV(pv2, 2, 0)
                evacV(pv3, 3, 0)
                nstep(3)
                chain_back(0, tok, KB[:, tok], b_KB[tg])
                nstep(3)
                for hp in range(2):
                    bank = psum.alloc()
                    mm(bank.ap, Wukv[:, hp * 128:(hp + 1) * 128], ckvn, True, True, [b_Wukv, b_ckvn], [bank.buf])
                    cp("dve", KC[0:64, 2 * hp, tok], bank.ap[0:64, :], [bank.buf], [b_KC[2 * hp][tg]])
                    cp("act", KC[0:64, 2 * hp + 1, tok], bank.ap[64:128, :], [bank.buf], [b_KC[2 * hp + 1][tg]])
                    psum.free(bank)
                    nstep(3)
                for t4 in range(4):
                    bank = psum.alloc()
                    mm(bank.ap[:, 0:256], ckvn[:, t4 * 128:(t4 + 1) * 128], Wukv[:, 256:512], True, True,
                       [b_Wukv, b_ckvn], [bank.buf])
                    evacV(bank, t4, 2)
                    nstep(2)
                drain(ng)

            T.barrier()
            if "attn" in flags:
                T.op("pool", lambda e: e.memset(Qz[:], 0.0), [], b_Qz)

            def q_norm(tg):
                b = tg % 2
                emit_norm(tg, g_attn, [HM[b][:, c, :] for c in range(8)], b_HM[b])

            def w_out(tg, fs):
                tok = slice(tg * TGN, (tg + 1) * TGN)
                b = tg % 2
                mixed = HM[b]
                for f in fs:
                    bank = psum.alloc()
                    for mc in range(8):
                        mm(bank.ap, Wo[:, mc, f * 128:(f + 1) * 128], mixed[:, mc, :], mc == 0, mc == 7,
                           [b_Wo, b_HM[b][mc]], [bank.buf])
                    tt("dve", XT[:, f, tok], XT[:, f, tok], bank.ap, ALU.add, [b_XT[f][tg], bank.buf], [b_XT[f][tg]])
                    psum.free(bank)

            def _projQ(tg, ci):
                b = tg % 2
                hT = HM[b]
                bank = psum.alloc()
                for kc in range(8):
                    mm(bank.ap, Wq[:, kc, ci * 128:(ci + 1) * 128], hT[:, kc, :], kc == 0, kc == 7,
                       [b_Wq, b_HM[b][kc]], [bank.buf])
                return bank

            def gen_proj(bank, tg, ci):
                b = tg % 2
                hT = HM[b]
                for kc in range(8):
                    mm(bank.ap, Wq[:, kc, ci * 128:(ci + 1) * 128], hT[:, kc, :], kc == 0, kc == 7,
                       [b_Wq, b_HM[b][kc]], [bank.buf])
                    if kc == 3:
                        yield
                yield

            def gen_qa(tg, c, state=None, key=None):
                while state is not None and not state[key]:
                    yield
                bank = psum.alloc()
                yield from gen_proj(bank, tg, c)
                T.op("pool", (lambda e, c=c: e.memset(Qbuf0[64:128, 2 * c, :], 0.0)), [], [b_Q0[2 * c]])
                T.op("pool", (lambda e, c=c: e.memset(Qbuf0[0:64, 2 * c + 1, :], 0.0)), [], [b_Q0[2 * c + 1]])
                yield from idle(2)
                cp("dve", Qbuf0[0:64, 2 * c, :], bank.ap[0:64, :], [bank.buf], [b_Q0[2 * c]])
                cp("dve", Qbuf0[64:128, 2 * c + 1, :], bank.ap[64:128, :], [bank.buf], [b_Q0[2 * c + 1]])
                psum.free(bank)
                yield

            def front_QA(tg, cs=(0, 1, 2)):
                for c in cs:
                    bank = _projQ(tg, c)
                    T.op("pool", (lambda e, c=c: e.memset(Qbuf0[64:128, 2 * c, :], 0.0)), [], [b_Q0[2 * c]])
                    T.op("pool", (lambda e, c=c: e.memset(Qbuf0[0:64, 2 * c + 1, :], 0.0)), [], [b_Q0[2 * c + 1]])
                    cp("dve", Qbuf0[0:64, 2 * c, :], bank.ap[0:64, :], [bank.buf], [b_Q0[2 * c]])
                    cp("dve", Qbuf0[64:128, 2 * c + 1, :], bank.ap[64:128, :], [bank.buf], [b_Q0[2 * c + 1]])
                    psum.free(bank)

            def gen_wout(prev):
                tok = slice(prev * TGN, (prev + 1) * TGN)
                b = prev % 2
                mixed = HM[b]
                for f in range(8):
                    bank = psum.alloc()
                    for mc in range(8):
                        mm(bank.ap, Wo[:, mc, f * 128:(f + 1) * 128], mixed[:, mc, :], mc == 0, mc == 7,
                           [b_Wo, b_HM[b][mc]], [bank.buf])
                        if mc == 3:
                            yield
                    yield from idle(3)
                    tt("dve", XT[:, f, tok], XT[:, f, tok], bank.ap, ALU.add, [b_XT[f][prev], bank.buf], [b_XT[f][prev]])
                    psum.free(bank)
                    yield

            def gen_qb_chain(tg, ci, si, state):
                tok = slice(tg * TGN, (tg + 1) * TGN)
                cs = chain[si]
                p = psum.alloc()
                yield from gen_proj(p, tg, 3 + ci)
                yield from idle(3)
                act(cs["sqb"], p.ap, AF.Square, [p.buf], [cs["b_sqb"]])
                yield from idle(3)
                ssb = psum.alloc()
                mm(ssb.ap, blockones, cs["sqb"], True, True, [cs["b_sqb"], b_cbf], [ssb.buf])
                yield from idle(3)
                act(cs["rs"], ssb.ap, AF.Ln, [ssb.buf], [cs["b_rs"]], scale=1.0 / 64, bias=EPS)
                psum.free(ssb)
                act(cs["rs"], cs["rs"], AF.Exp, [cs["b_rs"]], [cs["b_rs"]], scale=-0.5)
                yield from idle(4)
                stt("dve", cs["kn"], p.ap, smallp[:, g_bq:g_bq + 1], cs["rs"], ALU.mult, ALU.mult,
                    [p.buf, cs["b_rs"], b_cf32], [cs["b_kn"]])
                psum.free(p)
                yield from idle(3)
                rot = psum.alloc()
                mm(rot.ap, Rm, cs["kn"], True, True, [cs["b_kn"], b_cbf], [rot.buf])
                tt("pool", cs["t1"], cs["kn"], cosB[:, tok], ALU.mult, [cs["b_kn"], b_cbf], [cs["b_t1"]])
                yield from idle(3)
                tt("dve", cs["t2"], rot.ap, sinB[:, tok], ALU.mult, [rot.buf, b_cbf], [cs["b_t2"]])
                psum.free(rot)
                yield from idle(3)
                while not state["b_done"]:
                    yield
                tt("pool", Qz[0:64, 2 * ci, :], cs["t1"][0:64, :], cs["t2"][0:64, :], ALU.add,
                   [cs["b_t1"], cs["b_t2"]], [b_Qz[ci]])
                tt("dve", Qz[64:128, 2 * ci + 1, :], cs["t1"][64:128, :], cs["t2"][64:128, :], ALU.add,
                   [cs["b_t1"], cs["b_t2"]], [b_Qz[ci]])
                yield

            def gen_cq_chain(tg):
                pcq = []
                for c in range(2):
                    bk = psum.alloc()
                    pcq.append(bk)
                    yield from gen_proj(bk, tg, 6 + c)
                yield from idle(3)
                cqsq = sqh[0]
                act(cqsq[:, 0, :], pcq[0].ap, AF.Square, [pcq[0].buf], [b_sq[0]])
                act(cqsq[:, 1, :], pcq[1].ap, AF.Square, [pcq[1].buf], [b_sq[0]])
                yield from idle(4)
                sscq = psum.alloc()
                mm(sscq.ap, ones128, cqsq[:, 0, :], True, False, [b_sq[0], b_cbf], [sscq.buf])
                mm(sscq.ap, ones128, cqsq[:, 1, :], False, True, [b_sq[0], b_cbf], [sscq.buf])
                yield from idle(3)
                cs1 = chain[1]
                act(cs1["rs"], sscq.ap, AF.Ln, [sscq.buf], [cs1["b_rs"]], scale=1.0 / 256, bias=EPS)
                psum.free(sscq)
                act(cs1["rs"], cs1["rs"], AF.Exp, [cs1["b_rs"]], [cs1["b_rs"]], scale=-0.5)
                yield from idle(4)
                for c in range(2):
                    stt("dve", cqn[:, c, :], pcq[c].ap, smallp[:, g_cq + c:g_cq + c + 1], cs1["rs"],
                        ALU.mult, ALU.mult, [pcq[c].buf, cs1["b_rs"], b_cf32], [b_cqn])
                    psum.free(pcq[c])
                    yield

            def gen_cup(tg, hs):
                tok = slice(tg * TGN, (tg + 1) * TGN)
                banks = {}
                for h in hs:
                    bank = psum.alloc()
                    for c in range(2):
                        mm(bank.ap[0:96, :], Wuq[:, c, h * 96:(h + 1) * 96], cqn[:, c, :], c == 0, c == 1,
                           [b_Wuq, b_cqn], [bank.buf])
                    banks[h] = bank
                    yield
                yield from idle(2)
                for h in hs:
                    cp("dve", Qbuf0[0:96, h, :], banks[h].ap[0:96, :], [banks[h].buf], [b_Q0[h]])
                    psum.free(banks[h])
                    yield
                yield from idle(2)
                rqs = {}
                for h in hs:
                    cs = chain[h % 2]
                    rq = psum.alloc()
                    mm(rq.ap[0:96, :], Rm[0:96, 0:96], Qbuf0[0:96, h, :], True, True, [b_Q0[h], b_cbf], [rq.buf])
                    tt("pool", cs["t1"][64:96, :], Qbuf0[64:96, h, :], csC[64:96, tok], ALU.mult,
                       [b_Q0[h], b_cbf], [cs["b_t1"]])
                    rqs[h] = rq
                    yield
                yield from idle(2)
                for h in hs:
                    cs = chain[h % 2]
                    tt("dve", cs["t2"][64:96, :], rqs[h].ap[64:96, :], csC[0:32, tok], ALU.mult,
                       [rqs[h].buf, b_cbf], [cs["b_t2"]])
                    psum.free(rqs[h])
                    yield
                yield from idle(2)
                for h in hs:
                    cs = chain[h % 2]
                    tt("dve", Qbuf0[64:96, h, :], cs["t1"][64:96, :], cs["t2"][64:96, :], ALU.add,
                       [cs["b_t1"], cs["b_t2"]], [b_Q0[h]])
                    yield

            def gen_side(tg, state):
                s1 = gen_cup(tg, (0, 1))
                if tg + 1 < NTG:
                    nb = (tg + 1) % 2
                    s2 = par(gen_norm_d(tg + 1, g_attn, [HM[nb][:, c, :] for c in range(8)], b_HM[nb]),
                             gen_cup(tg, (2, 3)))
                    s3 = seq(gen_qb_chain(tg + 1, 0, 0, state), gen_qb_chain(tg + 1, 1, 1, state))
                    s4 = seq(gen_qb_chain(tg + 1, 2, 0, state), gen_cq_chain(tg + 1))
                    s5 = seq(gen_qa(tg + 1, 2), gen_qa(tg + 1, 0, state, "c01_done"))
                    return seq(s1, s2, s3, s4, s5)
                return seq(s1, gen_cup(tg, (2, 3)))

            def attn_A(tg, side=None):
                b = tg % 2
                mixed = HM[b]
                tiles = []
                for c in range(3):
                    for hf in range(2):
                        for qb in range(4):
                            tiles.append((c, hf, qb))
                NT = len(tiles)
                Sb = {}
                pis = {}
                Ob = {}

                def a_S(t):
                    c, hf, qb = tiles[t]
                    pr = slice(64 * hf, 64 * hf + 64)
                    n = 4 * tg + qb
                    jjs = [jj for jj in range(3) if 0 <= n - 1 + jj <= 15]
                    s_ = psum.alloc()
                    for jj in jjs:
                        kt = n - 1 + jj
                        mm(s_.ap[:, jj * 128:(jj + 1) * 128], KA[:, kt * 128:(kt + 1) * 128],
                           Qbuf0[:, 2 * c + hf, qb * 128:(qb + 1) * 128], True, True,
                           [b_KA[kt // 4], b_Q0[2 * c + hf]], [s_.buf])
                    Sb[t] = (s_, jjs, n)

                def a_exp(t):
                    c, hf, qb = tiles[t]
                    hA = c + 3 * hf
                    s_, jjs, n = Sb[t]
                    lo, hi = jjs[0] * 128, (jjs[-1] + 1) * 128
                    pi = ptc[0] % NPT
                    ptc[0] += 1
                    pis[t] = pi
                    act(PT[pi][:, lo:hi], s_.ap[:, lo:hi], AF.Exp, [s_.buf], [b_PT[pi]], scale=SC_AB)
                    psum.free(s_)
                    tt("dve", PT[pi][:, lo:hi], PT[pi][:, lo:hi], maskA[:, hA, lo:hi], ALU.mult,
                       [b_PT[pi], b_cbf], [b_PT[pi]])

                def a_pv(t):
                    c, hf, qb = tiles[t]
                    hA = c + 3 * hf
                    s_, jjs, n = Sb[t]
                    pi = pis[t]
                    if qb == 0:
                        Ob[(c, hf)] = psum.alloc("o")
                    O = Ob[(c, hf)]
                    for jj in jjs:
                        kt = n - 1 + jj
                        mm(O.ap[:, qb * 128:(qb + 1) * 128], vaug(0, kt, hf),
                           PT[pi][:, jj * 128:(jj + 1) * 128], (qb == 0 and jj == jjs[0]), jj == jjs[-1],
                           [b_V[kt // 4], b_Vones, b_PT[pi]], [O.buf], skip=True)
                    if qb == 3:
                        pend.append((t + 3, O, hf, c, hA))

                pend = []

                def a_norm(t):
                    while pend and pend[0][0] <= t:
                        _, O, hf, c, hA = pend.pop(0)
                        normalize(O, hf, mixed[:, c, :], b_HM[b][c], 0, sink_col=6 * l + hA)

                a_S(0)
                a_S(1)
                a_S(2)
                a_exp(0)
                a_exp(1)
                for t in range(NT):
                    if t + 3 < NT:
                        a_S(t + 3)
                    if t + 2 < NT:
                        a_exp(t + 2)
                    a_pv(t)
                    a_norm(t)
                    run_side(side, 2)
                a_norm(10 ** 9)
                drain(side)

            def gen_mid(tg, state):
                tok = slice(tg * TGN, (tg + 1) * TGN)
                for h in range(4):
                    bank = psum.alloc()
                    for c in range(2):
                        mm(bank.ap[0:96, :], Wuq[:, c, h * 96:(h + 1) * 96], cqn[:, c, :], c == 0, c == 1,
                           [b_Wuq, b_cqn], [bank.buf])
                    cp("dve", Qbuf0[0:96, h, :], bank.ap[0:96, :], [bank.buf], [b_Q0[h]])
                    psum.free(bank)
                    yield
                for h in range(4):
                    si = h % 2
                    cs = chain[si]
                    rq = psum.alloc()
                    mm(rq.ap[0:96, :], Rm[0:96, 0:96], Qbuf0[0:96, h, :], True, True, [b_Q0[h], b_cbf], [rq.buf])
                    tt("pool", cs["t1"][64:96, :], Qbuf0[64:96, h, :], csC[64:96, tok], ALU.mult,
                       [b_Q0[h], b_cbf], [cs["b_t1"]])
                    tt("dve", cs["t2"][64:96, :], rq.ap[64:96, :], csC[0:32, tok], ALU.mult, [rq.buf, b_cbf], [cs["b_t2"]])
                    psum.free(rq)
                    yield
                    tt("dve", Qbuf0[64:96, h, :], cs["t1"][64:96, :], cs["t2"][64:96, :], ALU.add,
                       [cs["b_t1"], cs["b_t2"]], [b_Q0[h]])
                    yield
                if tg + 1 < NTG:
                    yield from gen_norm(tg + 1, g_attn, [HM[(tg + 1) % 2][:, c, :] for c in range(8)], b_HM[(tg + 1) % 2])
                    yield from gen_front_rest(tg + 1, state)

            def attn_BC(tg, side=None, state=None):
                b = tg % 2
                mixed = HM[b]
                heads = []
                for c in range(3):
                    for hf in range(2):
                        heads.append(dict(kfun=(lambda j: KB[:, j * 128:(j + 1) * 128]), qap=Qz[:, 2 * c + hf, :],
                                          kbufs=b_KB, qbuf=b_Qz[c], vgrp=1, hf=hf, scale=SC_AB,
                                          dst=mixed[:, 3 + c, :], dbuf=b_HM[b][3 + c]))
                for h in range(4):
                    heads.append(dict(kfun=(lambda j, h=h: KC[0:96, h, j * 128:(j + 1) * 128]), qap=Qbuf0[0:96, h, :],
                                      kbufs=b_KC[h], qbuf=b_Q0[h], vgrp=2 + h // 2, hf=h % 2, scale=SC_C,
                                      dst=mixed[:, 6 + h // 2, :], dbuf=b_HM[b][6 + h // 2]))
                tiles = [(hi, j) for hi in range(len(heads)) for j in range(16)]
                NT = len(tiles)
                Sb = {}
                Ob = {}

                def issue_S(t):
                    hi, j = tiles[t]
                    h = heads[hi]
                    s_ = psum.alloc()
                    mm(s_.ap, h["kfun"](j), h["qap"], True, True, [h["kbufs"][j // 4], h["qbuf"]], [s_.buf])
                    Sb[t] = s_

                issue_S(0)
                issue_S(1)
                for t in range(NT):
                    hi, j = tiles[t]
                    h = heads[hi]
                    pi = ptc[0] % NPT
                    ptc[0] += 1
                    act(PT[pi], Sb[t].ap, AF.Exp, [Sb[t].buf], [b_PT[pi]], scale=h["scale"])
                    psum.free(Sb[t])
                    if t + 2 < NT:
                        issue_S(t + 2)
                    if j == 0:
                        Ob[hi] = psum.alloc("o")
                    O = Ob[hi]
                    pvop = mm(O.ap, vaug(h["vgrp"], j, h["hf"]), PT[pi], j == 0, j == 15,
                              [b_V[j // 4], b_Vones, b_PT[pi]], [O.buf])
                    pvop.hoist = False
                    if j == 15:
                        normalize(O, h["hf"], h["dst"], h["dbuf"], 0)
                    if t >= 94 and state is not None:
                        state["b_done"] = True
                    run_side(side, 1)
                if state is not None:
                    state["b_done"] = True
                    state["c01_done"] = True
                drain(side)

            if "attn" in flags:
                psum.split = True
                q_norm(0)
                st0 = {"b_done": True}
                drain(seq(gen_qb_chain(0, 0, 0, st0), gen_qb_chain(0, 1, 1, st0), gen_qb_chain(0, 2, 0, st0),
                          gen_cq_chain(0)))
                for tg in range(NTG):
                    front_QA(tg, (0, 1, 2) if tg == 0 else (1,))
                    attn_A(tg, gen_wout(tg - 1) if tg > 0 else None)
                    st = {"b_done": False, "c01_done": False}
                    attn_BC(tg, gen_side(tg, st), st)
                w_out(NTG - 1, range(8))
                psum.split = False

            T.barrier()

            def load_ffn(G):
                i = G % 2
                wload("w1" + "ab"[i], W1[i], w1_d[l, G], b_W1[i])
                wload("w2" + "ab"[i], W2[i], w2_d[l, G], b_W2[i])

            NGX = NG if "ffn" in flags else 0
            if NGX:
                load_ffn(0)
                load_ffn(1)
            if l + 1 < L and not NGX:
                prefetch_attn_weights(l + 1)
            for tg in range(NTG if NGX else 0):
                tok = slice(tg * TGN, (tg + 1) * TGN)
                emit_norm(tg, g_mlp, [h2T[:, c, tok] for c in range(8)], [b_h2T[c][tg] for c in range(8)])
            if NGX:
                dump("h2T", h2T, [b_h2T[c][g] for c in range(8) for g in range(NTG)])
            for G in range(NGX):
                i = G % 2
                for tg in range(NTG):
                    tok = slice(tg * TGN, (tg + 1) * TGN)
                    for j in range(4):
                        bank = psum.alloc()
                        for kc in range(8):
                            mm(bank.ap, W1[i][:, kc, j * 128:(j + 1) * 128], h2T[:, kc, tok], kc == 0, kc == 7,
                               [b_W1[i], b_h2T[kc][tg]], [bank.buf])
                        act(hidden[:, j, tok], bank.ap, AF.Relu, [bank.buf], [b_hid[j][tg]])
                        psum.free(bank)
                        tt("pool", hidden[:, j, tok], hidden[:, j, tok], hidden[:, j, tok], ALU.mult,
                           [b_hid[j][tg]], [b_hid[j][tg]])
                for tg in range(NTG):
                    tok = slice(tg * TGN, (tg + 1) * TGN)
                    for f in range(8):
                        bank = psum.alloc()
                        for j in range(4):
                            mm(bank.ap, W2[i][:, j, f * 128:(f + 1) * 128], hidden[:, j, tok], j == 0, j == 3,
                               [b_W2[i], b_hid[j][tg]], [bank.buf])
                        tt("dve", XT[:, f, tok], XT[:, f, tok], bank.ap, ALU.add, [b_XT[f][tg], bank.buf], [b_XT[f][tg]])
                        psum.free(bank)
                if G == 0:
                    dump("hidden0", hidden, [b_hid[j][g] for j in range(4) for g in range(NTG)])
                    dump("W1_0", W1[0], [b_W1[0]])
                    dump("W2_0", W2[0], [b_W2[0]])
                    dump("XT_G0", XT[:], [b_XT[c][g] for c in range(8) for g in range(NTG)])
                if G + 2 < NG:
                    load_ffn(G + 2)
                if G == 1 and l + 1 < L:
                    prefetch_attn_weights(l + 1, 0)
                if G == 3 and l + 1 < L:
                    prefetch_attn_weights(l + 1, 1)

            T.barrier()

        g_fin = CM["final"]
        oc = 0
        emit_norm(0, g_fin, [yT2[0][:, c, :] for c in range(8)], b_yT2[0])
        for tg in range(NTG):
            yT = yT2[tg % 2]
            b_yT = b_yT2[tg % 2]
            if tg + 1 < NTG:
                emit_norm(tg + 1, g_fin, [yT2[(tg + 1) % 2][:, c, :] for c in range(8)], b_yT2[(tg + 1) % 2])
            for t4 in range(4):
                k = oc % 2
                oc += 1
                for half in range(2):
                    bank = psum.alloc()
                    for i in range(4):
                        c = half * 4 + i
                        T.op("pe", (lambda e, bank=bank, i=i, c=c, t4=t4, yT=yT:
                                    e.transpose(bank.ap[:, i * 128:(i + 1) * 128], yT[:, c, t4 * 128:(t4 + 1) * 128], ident)),
                             [b_yT[c], b_cf32], [bank.buf])
                    cp("act" if half == 0 else "dve", ostage[k][:, half * 512:(half + 1) * 512], bank.ap,
                       [bank.buf], [b_os[k]])
                    psum.free(bank)
                rows = slice(tg * TGN + t4 * 128, tg * TGN + (t4 + 1) * 128)
                T.dma("sp", (lambda e, k=k, rows=rows: e.dma_start(out=out_d[rows, :], in_=ostage[k])),
                      dsem["os%d" % k], [b_os[k]], (), store=True)
        T.final_wait("sp")

        nw = T.resolve(esem)
        build_program.stats = dict(n_ops=len(T.ops), n_waits=nw)

        with nc.Block() as block:
            @block.tensor
            def _(e):
                T.emit("pe", e, esem)

            @block.scalar
            def _(e):
                T.emit("act", e, esem)

            @block.vector
            def _(e):
                T.emit("dve", e, esem)

            @block.gpsimd
            def _(e):
                T.emit("pool", e, esem)

            @block.sync
            def _(e):
                T.emit("sp", e, esem)
    return nc


def _const_tables():
    cb = np.zeros((128, CB_N), np.float32)
    cb[:, CB_ONES:CB_ONES + 128] = 1.0
    for h in range(2):
        cb[64 * h:64 * h + 64, CB_BLK + 64 * h:CB_BLK + 64 * h + 64] = 1.0
    for blk in range(4):
        o = 32 * blk
        for m in range(16):
            cb[o + m + 16, CB_RM + o + m] = -1.0
            cb[o + m + 16 - 16, CB_RM + o + m + 16] = 1.0
    slopes = 2.0 ** (-8.0 * np.arange(1, 7, dtype=np.float64) / 6)
    k = np.arange(128)[:, None]
    for h in range(6):
        for jj in range(3):
            q = np.arange(128)[None, :]
            dist = np.abs(q - k + (1 - jj) * 128).astype(np.float64)
            m = np.where(dist <= 128, np.exp(-slopes[h] * dist), 0.0)
            cb[:, CB_MASK + h * 384 + jj * 128:CB_MASK + h * 384 + (jj + 1) * 128] = m
    def tables(pos, dim):
        inv = (10000.0 ** (-np.arange(0, dim, 2, dtype=np.float32) / np.float32(dim))).astype(np.float32)
        ang = pos.astype(np.float32)[:, None] * inv[None, :]
        ang = np.concatenate([ang, ang], axis=-1)
        return np.cos(ang).astype(np.float32), np.sin(ang).astype(np.float32)
    t = np.arange(S)
    cr, sr = tables(t // 64, 32)
    cc, sc = tables(t % 64, 32)
    cm, sm = tables(t, 32)
    cosb = np.concatenate([cr, cc], axis=-1).T
    sinb = np.concatenate([sr, sc], axis=-1).T
    cb[0:64, CB_COSB:CB_COSB + S] = cosb
    cb[64:128, CB_COSB:CB_COSB + S] = cosb
    cb[0:64, CB_SINB:CB_SINB + S] = sinb
    cb[64:128, CB_SINB:CB_SINB + S] = sinb
    cb[64:96, CB_CSC:CB_CSC + S] = cm.T
    cb[0:32, CB_CSC:CB_CSC + S] = sm.T
    return cb


def _prep_weights(L, attn_norm, w_in, a_sink, b_q_norm, b_k_norm, c_q_norm, c_kv_norm,
                  w_uq, w_ukv, w_out, mlp_norm, w_ff1, w_ff2, final_norm):
    f = lambda a: np.asarray(a, dtype=np.float32)
    w_in, w_out, w_uq, w_ukv, w_ff1, w_ff2 = map(f, (w_in, w_out, w_uq, w_ukv, w_ff1, w_ff2))
    pair = []
    for c in range(3):
        pair += list(range(c * 64, c * 64 + 64)) + list(range((c + 3) * 64, (c + 3) * 64 + 64))
    pair = np.array(pair)
    colsQ = np.concatenate([pair, 640 + pair, np.arange(1280, 1536)])
    colsKV = np.concatenate([np.arange(384, 512), np.arange(1024, 1152), np.arange(1536, 1664),
                             np.arange(1664, 1696), np.arange(512, 640), np.arange(1152, 1280)])
    rowsO = np.concatenate([pair, 384 + pair, np.arange(768, 1024)])
    wq = np.ascontiguousarray(w_in[:L][:, :, colsQ].reshape(L, 8, 128, 1024).transpose(0, 2, 1, 3))
    wkv = np.ascontiguousarray(w_in[:L][:, :, colsKV].reshape(L, 8, 128, 672).transpose(0, 2, 1, 3))
    wo = np.ascontiguousarray(w_out[:L][:, rowsO, :].reshape(L, 8, 128, 1024).transpose(0, 2, 1, 3))
    wuq = np.ascontiguousarray(w_uq[:L].reshape(L, 2, 128, 384).transpose(0, 2, 1, 3))
    t_ = w_ukv[:L].reshape(L, 128, 4, 128)
    wukv = np.ascontiguousarray(np.concatenate([t_[:, :, :, 0:64].reshape(L, 128, 256),
                                                t_[:, :, :, 64:128].reshape(L, 128, 256)], axis=-1))
    w1 = np.ascontiguousarray(w_ff1[:L].reshape(L, 8, 128, NG, 512).transpose(0, 3, 2, 1, 4))
    w2 = np.ascontiguousarray(w_ff2[:L].reshape(L, NG, 4, 128, 1024).transpose(0, 1, 3, 2, 4))
    CM = _cols(L)
    sp = np.zeros((128, CM["_n"]), np.float32)
    an = f(attn_norm)[:L].reshape(L, 8, 128)
    mn = f(mlp_norm)[:L].reshape(L, 8, 128)
    for l in range(L):
        sp[:, CM["attn"] + 8 * l:CM["attn"] + 8 * l + 8] = an[l].T
        sp[:, CM["mlp"] + 8 * l:CM["mlp"] + 8 * l + 8] = mn[l].T
        sp[:, CM["bq"] + l] = np.tile(f(b_q_norm)[l], 2)
        sp[:, CM["bk"] + l] = np.tile(f(b_k_norm)[l], 2)
        sp[:, CM["cq"] + 2 * l:CM["cq"] + 2 * l + 2] = f(c_q_norm)[l].reshape(2, 128).T
        sp[:, CM["ckv"] + l] = f(c_kv_norm)[l]
        sp[:, CM["sink"] + 6 * l:CM["sink"] + 6 * l + 6] = np.broadcast_to(f(a_sink)[l][None, :], (128, 6))
    sp[:, CM["final"]:CM["final"] + 8] = f(final_norm).reshape(8, 128).T
    cf32 = np.concatenate([np.eye(128, dtype=np.float32), sp], axis=1)
    return dict(wq=wq, wkv=wkv, wo=wo, wuq=wuq, wukv=wukv, w1=w1, w2=w2,
                cbf=_const_tables(), cf32=np.ascontiguousarray(cf32))


_NC_CACHE = {}


def run(x, L=DEPTH, flags=("attn", "ffn"), dumps=(), **params):
    x = np.asarray(x, dtype=np.float32)
    shared = _prep_weights(L, **params)
    key = (L, tuple(flags), tuple(dumps))
    if key not in _NC_CACHE:
        _NC_CACHE[key] = build_program(L, flags, dumps)
    nc = _NC_CACHE[key]
    in_maps = []
    for i in range(NCORES):
        m = dict(shared)
        m["x"] = np.ascontiguousarray(x[i])
        in_maps.append(m)
    res = run_bass_kernel_spmd(nc, in_maps, core_ids=list(range(NCORES)))
    run.last_results = res.results
    return np.stack([np.asarray(r["out"], dtype=np.float32) for r in res.results], axis=0)


def kernel(x, attn_norm, w_in, a_sink, b_q_norm, b_k_norm, c_q_norm, c_kv_norm,
           w_uq, w_ukv, w_out, mlp_norm, w_ff1, w_ff2, final_norm):
    return run(x, DEPTH, attn_norm=attn_norm, w_in=w_in, a_sink=a_sink, b_q_norm=b_q_norm,
               b_k_norm=b_k_norm, c_q_norm=c_q_norm, c_kv_norm=c_kv_norm, w_uq=w_uq, w_ukv=w_ukv,
               w_out=w_out, mlp_norm=mlp_norm, w_ff1=w_ff1, w_ff2=w_ff2, final_norm=final_norm)
```
